# Optimizing a Trainium2 kernel written in Bass

```python
import jax, jax.numpy as jnp
from jax import lax
import numpy as np

D_MODEL = 1024
BATCH = 8
SEQ = 2048
DEPTH = 4

CHUNK = 128
A_WIDTH = D_MODEL
A_GROUPS = 8
A_GROUP_DIM = A_WIDTH // A_GROUPS
N_HEADS = 16
HEAD_DIM = 64
B_WIDTH = N_HEADS * HEAD_DIM
DILATED_PATTERNS = ((128, 1), (512, 4), (2048, 16))
BLOCK = 128
N_BRANCHES = 2
EPS = 1e-6
NEG_INF = -1e30
IN_COLS = 3 * A_WIDTH + 4 * B_WIDTH + N_BRANCHES * D_MODEL

kernel_name = "hybrid_gmlp_dilated_attn_block"


def rmsnorm(x, g):
    xf = x.astype(jnp.float32)
    y = xf * lax.rsqrt(jnp.mean(xf * xf, axis=-1, keepdims=True) + EPS)
    return (y * g.astype(jnp.float32)).astype(x.dtype)


def chunked_spatial_gating(u, v, w_s, b_s, g_v):
    Bn, S, _ = v.shape
    v = rmsnorm(v, g_v)
    vc = v.reshape(Bn, S // CHUNK, CHUNK, A_GROUPS, A_GROUP_DIM)
    causal = jnp.tril(jnp.ones((CHUNK, CHUNK), dtype=bool))
    w = jnp.where(causal[None], w_s, 0).astype(v.dtype)
    mixed = jnp.einsum('gts,bcsgd->bctgd', w, vc) + b_s.T[None, None, :, :, None]
    return u * mixed.reshape(Bn, S, A_WIDTH)


def dilated_pattern(q, k, v, slopes, window, dilation):
    Bn, H, S, Dh = q.shape
    L = S // dilation
    nback = window // dilation
    nb = -(-L // BLOCK)
    pad = nb * BLOCK - L

    def to_blocks(t):
        t = t.reshape(Bn, H, L, dilation, Dh).transpose(0, 1, 3, 2, 4)
        t = jnp.pad(t, ((0, 0), (0, 0), (0, 0), (0, pad), (0, 0)))
        return t.reshape(Bn, H, dilation, nb, BLOCK, Dh)

    qb, kb, vb = to_blocks(q), to_blocks(k), to_blocks(v)

    def with_prev(t):
        prev = jnp.pad(t, ((0, 0), (0, 0), (0, 0), (1, 0), (0, 0), (0, 0)))[:, :, :, :-1]
        return jnp.concatenate([prev, t], axis=4)

    kw, vw = with_prev(kb), with_prev(vb)
    qpos = jnp.arange(BLOCK)[:, None] + BLOCK
    kpos = jnp.arange(2 * BLOCK)[None, :]
    dist = qpos - kpos
    key_idx = (jnp.arange(nb)[:, None, None] - 1) * BLOCK + kpos[None]
    valid = (dist >= 0) & (dist <= nback) & (key_idx >= 0)

    s = jnp.einsum('bhrnqd,bhrnkd->bhrnqk', qb, kw) * (Dh ** -0.5)
    s = s - slopes[None, :, None, None, None, None] * (dist * dilation).astype(jnp.float32)
    s = jnp.where(valid[None, None, None], s, NEG_INF)
    m = jnp.max(s, axis=-1)
    p = jnp.exp(s - m[..., None])
    l = jnp.sum(p, axis=-1)
    o = jnp.einsum('bhrnqk,bhrnkd->bhrnqd', p, vw) / l[..., None]

    def from_blocks(t):
        t = t.reshape(Bn, H, dilation, nb * BLOCK, *t.shape[5:])[:, :, :, :L]
        t = jnp.moveaxis(t, 2, 3)
        return t.reshape(Bn, H, S, *t.shape[4:])

    return from_blocks(o), from_blocks(m), from_blocks(l)


def dilated_attention(q, k, v):
    Bn, S, _ = q.shape
    heads = lambda t: t.astype(jnp.float32).reshape(Bn, S, N_HEADS, HEAD_DIM).transpose(0, 2, 1, 3)
    qh, kh, vh = heads(q), heads(k), heads(v)
    slopes = 2.0 ** (-8.0 * jnp.arange(1, N_HEADS + 1, dtype=jnp.float32) / N_HEADS)
    outs = [dilated_pattern(qh, kh, vh, slopes, w, d) for (w, d) in DILATED_PATTERNS]
    m_all = jnp.max(jnp.stack([m for (_, m, _) in outs]), axis=0)
    alphas = [l * jnp.exp(m - m_all) for (_, m, l) in outs]
    num = sum(a[..., None] * o for a, (o, _, _) in zip(alphas, outs))
    o = num / sum(alphas)[..., None]
    return o.transpose(0, 2, 1, 3).reshape(Bn, S, B_WIDTH).astype(q.dtype)


def setup_inputs(seed: int = 0) -> dict:
    key = jax.random.key(seed)
    ks = jax.random.split(key, 10)
    f32 = jnp.float32
    x = jax.random.normal(ks[0], (BATCH, SEQ, D_MODEL), f32)
    g_norm = 1.0 + 0.05 * jax.random.normal(ks[1], (DEPTH, D_MODEL), f32)
    w_in = jax.random.normal(ks[2], (DEPTH, D_MODEL, IN_COLS), f32) * D_MODEL ** -0.5
    w_s = jax.random.normal(ks[3], (DEPTH, A_GROUPS, CHUNK, CHUNK), f32) * CHUNK ** -0.5
    b_s = 1.0 + 0.1 * jax.random.normal(ks[4], (DEPTH, A_GROUPS, CHUNK), f32)
    g_v = 1.0 + 0.05 * jax.random.normal(ks[5], (DEPTH, A_WIDTH), f32)
    w_proj_a = jax.random.normal(ks[6], (DEPTH, A_WIDTH, D_MODEL), f32) * A_WIDTH ** -0.5
    w_proj_b = jax.random.normal(ks[7], (DEPTH, B_WIDTH, D_MODEL), f32) * B_WIDTH ** -0.5
    w_out = jax.random.normal(ks[8], (DEPTH, D_MODEL, D_MODEL), f32) * D_MODEL ** -0.5
    g_final = 1.0 + 0.05 * jax.random.normal(ks[9], (D_MODEL,), f32)
    return {"x": x, "g_norm": g_norm, "w_in": w_in, "w_s": w_s, "b_s": b_s, "g_v": g_v,
            "w_proj_a": w_proj_a, "w_proj_b": w_proj_b, "w_out": w_out, "g_final": g_final}


def reference(x, g_norm, w_in, w_s, b_s, g_v, w_proj_a, w_proj_b, w_out, g_final):
    splits = [A_WIDTH, 2 * A_WIDTH, 3 * A_WIDTH,
              3 * A_WIDTH + B_WIDTH, 3 * A_WIDTH + 2 * B_WIDTH,
              3 * A_WIDTH + 3 * B_WIDTH, 3 * A_WIDTH + 4 * B_WIDTH]
    for layer in range(DEPTH):
        h = rmsnorm(x, g_norm[layer])
        proj = h @ w_in[layer]
        a_u, a_v, a_gate, q, k, v, b_gate, gate_logits = jnp.split(proj, splits, axis=-1)
        y_a = chunked_spatial_gating(jax.nn.gelu(a_u), jax.nn.gelu(a_v),
                                     w_s[layer], b_s[layer], g_v[layer]) * jax.nn.silu(a_gate)
        y_b = dilated_attention(q, k, v) * jax.nn.silu(b_gate)
        g_a, g_b = jnp.split(jax.nn.sigmoid(gate_logits), N_BRANCHES, axis=-1)
        merged = g_a * (y_a @ w_proj_a[layer]) + g_b * (y_b @ w_proj_b[layer])
        x = x + merged @ w_out[layer]
    return rmsnorm(x, g_final)
```

```python
import numpy as np
import ml_dtypes
import concourse.bass as bass
import concourse.mybir as mybir
from concourse.bass_utils import run_bass_kernel_spmd

F32 = mybir.dt.float32
BF16 = mybir.dt.bfloat16
AF = mybir.ActivationFunctionType
ALU = mybir.AluOpType

S = 2048
D = 1024
NL = 4
NH = 16
EPS = 1e-6
BLK = 512
WCOLS = 256
NSLOT = 4
BIG = 1.0e5


class Prog:
    ENG = ("pe", "act", "dve", "pool", "sp")

    def __init__(self, nc):
        self.nc = nc
        self.ins = {e: [] for e in self.ENG}
        self.lw = {}
        self.rd = {}
        self.groups = {}
        self.known = {e: {} for e in self.ENG}

    def _collect(self, eng, reads, writes):
        deps = {}

        def add(t):
            src, val = t
            if deps.get(src, -1) < val:
                deps[src] = val

        for k in reads:
            t = self.lw.get(k)
            if t is not None:
                add(t)
        for k in writes:
            t = self.lw.get(k)
            if t is not None:
                add(t)
            r = self.rd.get(k)
            if r:
                for src, val in r.items():
                    add((src, val))
        waits = []
        kn = self.known[eng]
        for src, val in deps.items():
            if src == ("e", "pe") and eng == "pe":
                continue
            if kn.get(src, -1) >= val:
                continue
            kn[src] = val
            waits.append((src, val))
            if src[0] == "e":
                self.ins[src[1]][val]["signal"] = True
        return waits

    def _update(self, tok, reads, writes):
        src, val = tok
        for k in writes:
            self.lw[k] = tok
            self.rd[k] = {}
        for k in reads:
            r = self.rd.setdefault(k, {})
            if r.get(src, -1) < val:
                r[src] = val

    def op(self, eng, fn, reads=(), writes=()):
        waits = self._collect(eng, reads, writes)
        idx = len(self.ins[eng])
        self.ins[eng].append(dict(fn=fn, waits=waits, signal=False, dma=None))
        self._update((("e", eng), idx), reads, writes)

    def dma(self, queue, group, fn, reads=(), writes=()):
        gk = ("dg", group)
        waits = self._collect(queue, tuple(reads), tuple(writes) + (gk,))
        cnt = self.groups.get(group, 0) + 16
        self.groups[group] = cnt
        self.ins[queue].append(dict(fn=fn, waits=waits, signal=False, dma=group))
        self._update((("d", group), cnt), tuple(reads), tuple(writes) + (gk,))

    def emit(self, final_waits):
        nc = self.nc
        import contextlib
        with contextlib.ExitStack() as st:
            sems = {}
            for e in self.ENG:
                sems[("e", e)] = st.enter_context(nc.semaphore("s_" + e))
            for g in self.groups:
                sems[("d", g)] = st.enter_context(nc.semaphore("d_" + g))
            cnts = {}
            for e in self.ENG:
                c = 0
                arr = []
                for ins in self.ins[e]:
                    if ins["signal"]:
                        c += 1
                    arr.append(c)
                cnts[e] = arr
            block = st.enter_context(nc.Block())

            def run(e, eng):
                for ins in self.ins[e]:
                    for src, val in ins["waits"]:
                        v = cnts[src[1]][val] if src[0] == "e" else val
                        eng.wait_ge(sems[src], v)
                    r = ins["fn"](eng)
                    if ins["dma"] is not None:
                        r.then_inc(sems[("d", ins["dma"])], 16)
                    elif ins["signal"]:
                        r.then_inc(sems[("e", e)], 1)
                if e == "sp":
                    for g in final_waits:
                        eng.wait_ge(sems[("d", g)], self.groups[g])

            block.tensor(lambda eng: run("pe", eng))
            block.scalar(lambda eng: run("act", eng))
            block.vector(lambda eng: run("dve", eng))
            block.gpsimd(lambda eng: run("pool", eng))
            block.sync(lambda eng: run("sp", eng))


class Arena:
    def __init__(self, nc, nbytes):
        self.t = nc.alloc_sbuf_tensor("arena", [128, nbytes // 2], BF16)
        self.top = 0
        self.cap = nbytes

    def alloc(self, nbytes):
        off = self.top
        self.top += (nbytes + BLK - 1) // BLK * BLK
        assert self.top <= self.cap, (self.top, self.cap)
        return off


class Buf:
    def __init__(self, arena, shape, dt, at=None):
        self.es = 4 if dt == F32 else 2
        n = int(np.prod(shape))
        self.nbytes = n * self.es
        self.off = arena.alloc(self.nbytes) if at is None else at
        v = arena.t[:, self.off // 2:(self.off + self.nbytes) // 2]
        if dt == F32:
            v = v.bitcast(F32)
        if len(shape) == 2:
            v = v.rearrange("p (a b) -> p a b", a=shape[0])
        elif len(shape) == 3:
            v = v.rearrange("p (a b c) -> p a b c", a=shape[0], b=shape[1])
        self.ap = v
        self.shape = tuple(shape)

    def k(self, lo=0, hi=None):
        if hi is None:
            hi = self.nbytes // self.es
        b0 = (self.off + lo * self.es) // BLK
        b1 = (self.off + hi * self.es - 1) // BLK
        return tuple(("sb", b) for b in range(b0, b1 + 1))

    def k2(self, i, lo=0, hi=None):
        n = int(np.prod(self.shape[1:]))
        if hi is None:
            hi = n
        return self.k(i * n + lo, i * n + hi)


def build(l0, l1, final):
    nc = bass.Bass("TRN2", target_bir_lowering=False)
    dr = lambda name, shape, dt, kind: nc.dram_tensor(name, shape, dt, kind=kind).ap()
    x_d = dr("x", [S, D], F32, "ExternalInput")
    out_d = dr("out", [S, D], F32, "ExternalOutput")
    w_in = dr("w_in", [NL, D, 9216], F32, "ExternalInput")
    w_pa = dr("w_pa", [NL, D, D], F32, "ExternalInput")
    w_pb = dr("w_pb", [NL, D, D], F32, "ExternalInput")
    w_o = dr("w_o", [NL, D, D], F32, "ExternalInput")
    w_s = dr("w_s", [NL, 8, 128, 128], F32, "ExternalInput")
    bs_d = dr("bs", [NL, 1, 1024], F32, "ExternalInput")
    gvb_d = dr("gvb", [NL, 128, 1024], F32, "ExternalInput")
    gn_d = dr("gn", [128, NL * 8], F32, "ExternalInput")
    gf_d = dr("gf", [128, 8], F32, "ExternalInput")
    id_d = dr("ident", [128, 128], F32, "ExternalInput")
    tri_d = dr("tri", [128, 128], F32, "ExternalInput")
    dist_d = dr("dist", [128, 256], F32, "ExternalInput")
    vd = dr("vd", [S, 8, 192], BF16, "Internal")

    P = Prog(nc)
    A = Arena(nc, 212480)
    XT = Buf(A, [8, S], F32)
    HT = Buf(A, [8, S], BF16)
    VY = Buf(A, [8, S], BF16)
    r_off = A.alloc(34 * 1024)
    QT = [Buf(A, [S], BF16, at=r_off + i * 4096) for i in range(2)]
    KT = [Buf(A, [S], BF16, at=r_off + 8192 + i * 4096) for i in range(2)]
    VL0 = [Buf(A, [16, 192], BF16, at=r_off + 16384)] * 2
    VL1 = Buf(A, [16, 192], BF16, at=r_off + 16384 + 6144)
    VL2 = Buf(A, [16, 192], BF16, at=r_off + 16384 + 12288)

    def v_stat(vbuf, tile, hh):
        return vbuf.ap[:, tile, 0:128] if hh == 0 else vbuf.ap[:, tile, 64:192]
    M = Buf(A, [8, S], BF16, at=r_off)
    QT4 = Buf(A, [S], BF16)
    WS_ = [Buf(A, [8, WCOLS], BF16) for _ in range(NSLOT)]
    GN = Buf(A, [NL * 8], F32)
    GF = Buf(A, [8], F32)
    GVB = Buf(A, [1024], F32)
    WSM = Buf(A, [8, 128], BF16)
    BSR = Buf(A, [1024], BF16)
    IDT = Buf(A, [128], F32)
    TRI = Buf(A, [128], F32)
    DIST = Buf(A, [256], F32)
    ONES = Buf(A, [128], BF16)
    t_off = A.alloc(12 * 1024)
    tmp = lambda shape, dt, o: Buf(A, shape, dt, at=t_off + o)
    XS = [Buf(A, [1024], F32, at=VY.off + i * 4096) for i in range(8)]
    SQ = [tmp([512], BF16, i * 1024) for i in range(2)]
    RSTD = [tmp([512], F32, 2048 + i * 2048) for i in range(2)]
    VST = [tmp([4, 192], BF16, i * 1536) for i in range(6)]
    PT = [tmp([512], BF16, i * 1024) for i in range(4)] + [tmp([512], BF16, 10752)]
    DD = [tmp([512], F32, 4096 + i * 2048) for i in range(2)]
    MK = tmp([2, 2, 256], BF16, 8192)
    MK3 = tmp([2, 128], BF16, 8192 + 2048)
    T1 = [tmp([512], BF16, i * 1024) for i in range(4)]
    GG = [tmp([1024], F32, i * 4096) for i in range(2)]
    SSQ = tmp([16], F32, 8192)
    UU2 = [tmp([S], BF16, i * 4096) for i in range(2)]
    SGT = [tmp([512], BF16, 8704 + i * 1024) for i in range(2)]
    WST = tmp([8, 128], F32, 8192)
    WSC = [tmp([128], BF16, 10752 + i * 512) for i in range(3)]
    wsc_i = [0]
    PS = [nc.alloc_psum_tensor("ps%d" % i, [128, 512], F32) for i in range(8)]
    psk = lambda b: (("ps", b),)

    def mm(out, lhsT, rhs, start, stop, reads, writes):
        P.op("pe", lambda e: e.matmul(out, lhsT=lhsT, rhs=rhs, start=start, stop=stop,
                                      skip_group_check=True), reads, writes)

    def act(out, in_, func, reads, writes, scale=1.0, accum_out=None):
        if accum_out is None:
            P.op("act", lambda e: e.activation(out=out, in_=in_, func=func, scale=scale), reads, writes)
        else:
            P.op("act", lambda e: e.activation(out=out, in_=in_, func=func, scale=scale,
                                               accum_out=accum_out), reads, writes)

    def tt(out, in0, in1, op, reads, writes, eng="dve"):
        P.op(eng, lambda e: e.tensor_tensor(out=out, in0=in0, in1=in1, op=op), reads, writes)

    def tcopy(out, in_, reads, writes, eng="dve"):
        P.op(eng, lambda e: e.tensor_copy(out=out, in_=in_), reads, writes)

    chunks = []
    for l in range(l0, l1):
        cw = lambda base, j, l=l: w_in[l, :, base + j * WCOLS: base + (j + 1) * WCOLS]
        for j in range(4):
            chunks.append(cw(5120, j))
        for j in range(4):
            chunks.append(cw(3072, j))
            chunks.append(cw(4096, j))
        for j in range(4):
            chunks.append(cw(6144, j))
        for j in range(4):
            chunks.append(w_pb[l, :, j * WCOLS:(j + 1) * WCOLS])
            chunks.append(cw(8192, j))
        for j in range(4):
            chunks.append(cw(1024, j))
        for j in range(4):
            chunks.append(cw(0, j))
            chunks.append(cw(2048, j))
        for j in range(4):
            chunks.append(w_pa[l, :, j * WCOLS:(j + 1) * WCOLS])
            chunks.append(cw(7168, j))
        for j in range(4):
            chunks.append(w_o[l, :, j * WCOLS:(j + 1) * WCOLS])
    wstate = dict(next_load=0, next_use=0)

    def w_load():
        i = wstate["next_load"]
        if i >= len(chunks):
            return
        wstate["next_load"] = i + 1
        slot = WS_[i % NSLOT]
        src = chunks[i].rearrange("(kt p) f -> p kt f", p=128)
        P.dma("pool", "w%d" % (i % NSLOT), lambda e: e.dma_start(out=slot.ap, in_=src),
              reads=(), writes=slot.k())

    def w_acquire():
        i = wstate["next_use"]
        wstate["next_use"] = i + 1
        return WS_[i % NSLOT]

    def w_release():
        w_load()

    P.dma("sp", "c0", lambda e: e.dma_start(out=GN.ap, in_=gn_d), writes=GN.k())
    P.dma("sp", "c1", lambda e: e.dma_start(out=GF.ap, in_=gf_d), writes=GF.k())
    P.dma("sp", "c2", lambda e: e.dma_start(out=IDT.ap, in_=id_d), writes=IDT.k())
    P.dma("sp", "c3", lambda e: e.dma_start(out=TRI.ap, in_=tri_d), writes=TRI.k())
    P.dma("sp", "c4", lambda e: e.dma_start(out=DIST.ap, in_=dist_d), writes=DIST.k())
    P.op("dve", lambda e: e.memset(ONES.ap, 1.0), writes=ONES.k())
    for _ in range(NSLOT):
        w_load()

    def rmsnorm_stats(s, rs, bank=None):
        if bank is None:
            bank = s % 4
        for ct in range(8):
            sq = SQ[ct % 2]
            act(sq.ap, XT.ap[:, ct, s * 512:(s + 1) * 512], AF.Square,
                XT.k2(ct, s * 512, (s + 1) * 512), sq.k())
            mm(PS[bank][:, :], ONES.ap, sq.ap, ct == 0, ct == 7, ONES.k() + sq.k(), psk(bank))
        P.op("dve", lambda e: e.tensor_scalar(out=rs.ap, in0=PS[bank][:, :], scalar1=1.0 / D, scalar2=EPS,
                                              op0=ALU.mult, op1=ALU.add), psk(bank), rs.k())
        act(rs.ap, rs.ap, AF.Ln, rs.k(), rs.k())
        act(rs.ap, rs.ap, AF.Exp, rs.k(), rs.k(), scale=-0.5)

    def proj_fm(wslot, c0, banks, evac):
        for s in range(4):
            for ct in range(8):
                mm(PS[banks[s]][:, :], wslot.ap[:, ct, c0:c0 + 128], HT.ap[:, ct, s * 512:(s + 1) * 512],
                   ct == 0, ct == 7, wslot.k2(ct) + HT.k2(ct, s * 512, (s + 1) * 512), psk(banks[s]))
            evac(s, banks[s])

    def p0_span(l_next, s, ssbank):
        rs = RSTD[s % 2]
        rmsnorm_stats(s, rs, ssbank)
        for ct in range(8):
            xk = XT.k2(ct, s * 512, (s + 1) * 512)
            if l_next is not None:
                P.op("dve", lambda e, ct=ct, rs=rs: e.scalar_tensor_tensor(
                    out=HT.ap[:, ct, s * 512:(s + 1) * 512], in0=XT.ap[:, ct, s * 512:(s + 1) * 512],
                    scalar=GN.ap[:, (l_next * 8 + ct):(l_next * 8 + ct) + 1], in1=rs.ap, op0=ALU.mult, op1=ALU.mult),
                    xk + GN.k() + rs.k(), HT.k2(ct, s * 512, (s + 1) * 512))
            else:
                P.op("dve", lambda e, ct=ct, rs=rs: e.scalar_tensor_tensor(
                    out=XT.ap[:, ct, s * 512:(s + 1) * 512], in0=XT.ap[:, ct, s * 512:(s + 1) * 512],
                    scalar=GF.ap[:, ct:ct + 1], in1=rs.ap, op0=ALU.mult, op1=ALU.mult), xk + GF.k() + rs.k(), xk)

    for i in range(16):
        xs = XS[i % 8]
        P.dma("sp", "xs%d" % (i % 8), lambda e, xs=xs, i=i: e.dma_start(out=xs.ap, in_=x_d[i * 128:(i + 1) * 128, :]),
              writes=xs.k())
        for half in range(2):
            bank = (i * 2 + half) % 2
            for c4 in range(4):
                ct = half * 4 + c4
                P.op("pe", lambda e, bank=bank, c4=c4, xs=xs, ct=ct: e.transpose(
                    PS[bank][:, c4 * 128:(c4 + 1) * 128], xs.ap[:, ct * 128:(ct + 1) * 128], IDT.ap),
                    reads=xs.k(ct * 128, (ct + 1) * 128) + IDT.k(), writes=psk(bank))
            dst = XT.ap[:, half * 4:half * 4 + 4, i * 128:(i + 1) * 128]
            src = PS[bank][:, :].rearrange("p (a b) -> p a b", a=4)
            wk = ()
            for c4 in range(4):
                wk += XT.k2(half * 4 + c4, i * 128, (i + 1) * 128)
            if half == 0:
                tcopy(dst, src, psk(bank), wk, eng="dve")
            else:
                act(dst, src, AF.Copy, psk(bank), wk)
        if i % 4 == 3:
            p0_span(l0, i // 4, 4 + (i // 4) % 4)

    slopes = [2.0 ** (-8.0 * (h + 1) / NH) for h in range(NH)]
    bankset = [0]

    def next_banks():
        b = bankset[0]
        bankset[0] = 4 - b
        return [b, b + 1, b + 2, b + 3]

    for l in range(l0, l1):
        P.dma("sp", "lp0", lambda e, l=l: e.dma_start(out=GVB.ap, in_=gvb_d[l]), writes=GVB.k())
        P.dma("pool", "lpb", lambda e, l=l: e.dma_start(out=BSR.ap[0:1, :], in_=bs_d[l]), writes=BSR.k())
        P.dma("sp", "lp1", lambda e, l=l: e.dma_start(out=WST.ap, in_=w_s[l].rearrange("g t s -> t g s")),
              writes=WST.k())
        for g in range(8):
            bank = 4 + (g % 4)
            P.op("pe", lambda e, g=g, bank=bank: e.transpose(PS[bank][:, 0:128], WST.ap[:, g, :], IDT.ap),
                 WST.k() + IDT.k(), psk(bank))
            tt(WSM.ap[:, g, :], PS[bank][:, 0:128], TRI.ap, ALU.mult, psk(bank) + TRI.k(), WSM.k2(g))

        for half in range(2):
            wv2 = [w_acquire(), w_acquire()]
            for i in range(16):
                bank = 4 + i % 4
                vst = VST[i % 6]
                if half == 0 and i < 6:
                    P.op("dve", lambda e, vst=vst: e.memset(vst.ap[:, :, 64:128], 1.0), writes=vst.k())
                for j in range(2):
                    for ct in range(8):
                        mm(PS[bank][:, j * 256:(j + 1) * 256], HT.ap[:, ct, i * 128:(i + 1) * 128],
                           wv2[j].ap[:, ct, :], ct == 0, ct == 7,
                           HT.k2(ct, i * 128, (i + 1) * 128) + wv2[j].k2(ct), psk(bank))
                src = PS[bank][:, :].rearrange("p (a h c) -> p a h c", a=4, h=2)
                dst = vst.ap.rearrange("p a (h c) -> p a h c", h=3)[:, :, 0:3:2, :]
                if i % 2 == 0:
                    tcopy(dst, src, psk(bank), vst.k())
                else:
                    act(dst, src, AF.Copy, psk(bank), vst.k())
                P.dma("sp", "vst%d" % (i % 6), lambda e, vst=vst, i=i, half=half: e.dma_start(
                    out=vd[i * 128:(i + 1) * 128, half * 4:(half + 1) * 4, :], in_=vst.ap),
                    reads=vst.k(), writes=tuple(("vd", hp_, i) for hp_ in range(half * 4, half * 4 + 4)))
            w_release()
            w_release()

        def load_v0(hp):
            rk = tuple(("vd", hp, i_) for i_ in range(16))
            vb = VL0[hp % 2]
            P.dma("sp", "vl0", lambda e: e.dma_start(
                out=vb.ap, in_=vd[:, hp, :].rearrange("(i p) c -> p i c", p=128)), reads=rk, writes=vb.k())

        def load_v1(hp):
            rk = tuple(("vd", hp, i_) for i_ in range(16))
            P.dma("sp", "vl1", lambda e: e.dma_start(
                out=VL1.ap.rearrange("p (r b) c -> p r b c", r=4),
                in_=vd[:, hp, :].rearrange("(b p r) c -> p r b c", b=4, p=128, r=4)), reads=rk, writes=VL1.k())

        def load_v2(hp):
            rk = tuple(("vd", hp, i_) for i_ in range(16))
            P.dma("sp", "vl2", lambda e: e.dma_start(
                out=VL2.ap, in_=vd[:, hp, :].rearrange("(p r) c -> p r c", r=16)), reads=rk, writes=VL2.k())

        qk_slots = {}

        def qk_item(hp, which, s, qb=7):
            j = hp // 2
            if (hp % 2 == 0) and which == 0 and s == 0:
                qk_slots[j] = (w_acquire(), w_acquire())
            wsl = qk_slots[j][which]
            c0 = (hp % 2) * 128
            dst = (QT if which == 0 else KT)[hp % 2]
            for ct in range(8):
                mm(PS[qb][:, :], wsl.ap[:, ct, c0:c0 + 128], HT.ap[:, ct, s * 512:(s + 1) * 512],
                   ct == 0, ct == 7, wsl.k2(ct) + HT.k2(ct, s * 512, (s + 1) * 512), psk(qb))
            tcopy(dst.ap[:, s * 512:(s + 1) * 512], PS[qb][:, :], psk(qb), dst.k(s * 512, (s + 1) * 512))
            if (hp % 2 == 1) and which == 1 and s == 3:
                w_release()
                w_release()

        items = [(w, s) for s in range(4) for w in range(2)]
        for ii, (w, s) in enumerate(items):
            qk_item(0, w, s, ii)

        load_v0(0)
        load_v2(0)
        load_v1(0)
        tasks = []
        first_flag = {}
        for hp in range(8):
            qt, kt = QT[hp % 2], KT[hp % 2]
            vl0 = VL0[hp % 2]

            def gen_masks(hp=hp, qt=qt):
                P.op("dve", lambda e, qt=qt: e.tensor_copy(
                    out=QT4.ap.rearrange("p (b r j) -> p b r j", b=4, r=4),
                    in_=qt.ap.rearrange("p (b j r) -> p b r j", b=4, r=4)), qt.k(), QT4.k())
                for hh in range(2):
                    sl = slopes[2 * hp + hh]
                    for pi, dil in enumerate((1, 4)):
                        act(MK.ap[:, hh, pi, :], DIST.ap, AF.Exp, DIST.k(), MK.k(), scale=-sl * dil)
                    act(MK3.ap[:, hh, :], DIST.ap[:, 128:256], AF.Exp, DIST.k(), MK3.k(), scale=-sl * 16)
            ptasks = []
            for b in range(4):
                obank = {0: (b % 2) * 2, 1: (b % 2) * 2 + 1}
                for hh in range(2):
                    first_flag[(hp, b, hh)] = True

                def pv(hh, vbuf, tile, ptile, pcols, ocols, obank=obank, hp=hp, b=b):
                    ob = obank[hh]
                    fkey = (hp, b, hh)
                    mm(PS[ob][:, ocols], v_stat(vbuf, tile, hh), ptile.ap[:, pcols],
                       first_flag[fkey], False, vbuf.k() + ptile.k(), psk(ob))
                    first_flag[fkey] = False

                work = []
                for jq in range(4):
                    j = 4 * b + jq
                    qsl = slice(j * 128, (j + 1) * 128)
                    ksl_prev = slice((j - 1) * 128, j * 128) if j > 0 else None
                    work.append((0, qsl, ksl_prev, qsl, (vl0, j - 1), (vl0, j), slice(jq * 128, (jq + 1) * 128)))
                for r in range(4):
                    qsl = slice(512 * b + r, 512 * (b + 1), 4)
                    ksl_prev = slice(512 * (b - 1) + r, 512 * b, 4) if b > 0 else None
                    work.append((1, slice(512 * b + 128 * r, 512 * b + 128 * (r + 1)), ksl_prev, qsl,
                                 (VL1, r * 4 + b - 1), (VL1, r * 4 + b), slice(r, 512, 4)))
                for wi in range(0, 8, 2):
                    for hh in range(2):
                        def s_fn(tk, wi=wi, work=work, b=b, qt=qt, kt=kt, hh=hh):
                            sb, ptile = tk["sb"], tk["pt"]
                            pb = hh * 64
                            pat = work[wi][0]
                            for u in range(2):
                                _, qsl, kprev, kcur, vprev, vcur, ocols = work[wi + u]
                                qsrc = QT4 if pat == 1 else qt
                                qrd = qsrc.k(qsl.start, qsl.stop)
                                if kprev is not None:
                                    mm(PS[sb][:, u * 256:u * 256 + 128], kt.ap[pb:pb + 64, kprev], qsrc.ap[pb:pb + 64, qsl],
                                       True, True, kt.k(max(0, 512 * (b - 1)), 512 * (b + 1)) + qrd, psk(sb))
                                mm(PS[sb][:, u * 256 + 128:u * 256 + 256], kt.ap[pb:pb + 64, kcur], qsrc.ap[pb:pb + 64, qsl],
                                   True, True, kt.k(512 * b, 512 * (b + 1)) + qrd, psk(sb))
                            act(ptile.ap, PS[sb][:, :], AF.Exp, psk(sb), ptile.k(), scale=0.125)
                            mk = MK.ap[:, hh, pat, :].rearrange("p (o c) -> p o c", o=1).broadcast_to([128, 2, 256])
                            p3 = ptile.ap.rearrange("p (a c) -> p a c", a=2)
                            tt(p3, p3, mk, ALU.mult, ptile.k() + MK.k(), ptile.k())

                        def pv_fn(tk, wi=wi, work=work, pv=pv, hh=hh):
                            ptile = tk["pt"]
                            for u in range(2):
                                _, qsl, kprev, kcur, vprev, vcur, ocols = work[wi + u]
                                if kprev is not None:
                                    pv(hh, vprev[0], vprev[1], ptile, slice(u * 256, u * 256 + 128), ocols)
                                pv(hh, vcur[0], vcur[1], ptile, slice(u * 256 + 128, u * 256 + 256), ocols)
                        tkw = dict(s=s_fn, pv=pv_fn, before=[], after=[])
                        ptasks.append(tkw)
                        if b == 3 and wi == 2 and hh == 1 and hp + 1 < 8:
                            tkw["after"].append(lambda hp=hp: load_v0(hp + 1))

                for hh in range(2):
                    def s3_fn(tk, hh=hh, b=b, qt=qt, kt=kt):
                        sb, ptile = tk["sb"], tk["pt"]
                        pb = hh * 64
                        for r in range(16):
                            mm(PS[sb][:, r * 32:(r + 1) * 32], kt.ap[pb:pb + 64, r:S:16],
                               qt.ap[pb:pb + 64, 512 * b + r:512 * (b + 1):16], True, True,
                               kt.k() + qt.k(512 * b, 512 * (b + 1)), psk(sb))
                        act(ptile.ap, PS[sb][:, :], AF.Exp, psk(sb), ptile.k(), scale=0.125)
                        mk = MK3.ap[:, hh, 32 * b:32 * (b + 1)].rearrange("p (o c) -> p o c", o=1).broadcast_to([128, 16, 32])
                        p3 = ptile.ap.rearrange("p (a c) -> p a c", a=16)
                        tt(p3, p3, mk, ALU.mult, ptile.k() + MK3.k(), ptile.k())

                    def pv3_fn(tk, hh=hh, pv=pv):
                        ptile = tk["pt"]
                        for r in range(16):
                            pv(hh, VL2, r, ptile, slice(r * 32, (r + 1) * 32), slice(r, 512, 16))
                    tk3 = dict(s=s3_fn, pv=pv3_fn, before=[], after=[])
                    ptasks.insert(len(ptasks) - (4 if (hh == 0 or b == 3) else 0), tk3)
                    if b == 3 and hh == 1 and hp + 1 < 8:
                        tk3["after"].append(lambda hp=hp: load_v2(hp + 1))

                def norm_fn(b=b, obank=obank, hp=hp):
                    dd = DD[b % 2]
                    oa, obb = obank[0], obank[1]
                    act(dd.ap[0:64, :], PS[oa][64:128, :], AF.Ln, psk(oa), dd.k())
                    act(dd.ap[64:128, :], PS[obb][0:64, :], AF.Ln, psk(obb), dd.k())
                    act(dd.ap, dd.ap, AF.Exp, dd.k(), dd.k(), scale=-1.0)
                    tt(VY.ap[0:64, hp, b * 512:(b + 1) * 512], PS[oa][0:64, :], dd.ap[0:64, :], ALU.mult,
                       psk(oa) + dd.k(), VY.k2(hp, b * 512, (b + 1) * 512))
                    tt(VY.ap[64:128, hp, b * 512:(b + 1) * 512], PS[obb][64:128, :], dd.ap[64:128, :], ALU.mult,
                       psk(obb) + dd.k(), VY.k2(hp, b * 512, (b + 1) * 512))
                ptasks[-1]["after"].append(norm_fn)
            ptasks[0]["before"].append(gen_masks)
            if hp + 1 < 8:
                for ii, (w_, s_) in reversed(list(enumerate(items))):
                    ptasks.insert(5 * ii + 5, dict(
                        s=lambda tk, hp=hp, w_=w_, s_=s_: qk_item(hp + 1, w_, s_, tk["sb"]),
                        pv=lambda tk: None, before=[], after=[]))
                ptasks[-1]["after"].append(lambda hp=hp: load_v1(hp + 1))
            tasks += ptasks

        SKEW = 3
        for t in range(len(tasks) + SKEW):
            if t < len(tasks):
                tk = tasks[t]
                tk["sb"] = 4 + t % 4
                tk["pt"] = PT[t % 5]
                for f in tk["before"]:
                    f()
                tk["s"](tk)
            if t >= SKEW:
                tk = tasks[t - SKEW]
                tk["pv"](tk)
                for f in tk["after"]:
                    f()

        ti = [0]
        for j in range(4):
            wsl = w_acquire()
            for f2 in range(2):
                hp = 2 * j + f2

                def ev(s, bank, hp=hp):
                    t1 = T1[ti[0] % 4]
                    ti[0] += 1
                    act(t1.ap, PS[bank][:, :], AF.Silu, psk(bank), t1.k())
                    tt(VY.ap[:, hp, s * 512:(s + 1) * 512], VY.ap[:, hp, s * 512:(s + 1) * 512], t1.ap, ALU.mult,
                       VY.k2(hp, s * 512, (s + 1) * 512) + t1.k(), VY.k2(hp, s * 512, (s + 1) * 512))
                proj_fm(wsl, f2 * 128, next_banks(), ev)
            w_release()

        def proj_merge(accumulate):
            for j in range(4):
                wp = w_acquire()
                wg = w_acquire()
                for f2 in range(2):
                    n = 2 * j + f2
                    gb = next_banks()
                    sig = []

                    def ev_g(s, bank):
                        t1 = T1[ti[0] % 4]
                        ti[0] += 1
                        act(t1.ap, PS[bank][:, :], AF.Sigmoid, psk(bank), t1.k())
                        sig.append(t1)
                    pbanks = next_banks()
                    for s in range(4):
                        for ct in range(8):
                            mm(PS[gb[s]][:, :], wg.ap[:, ct, f2 * 128:(f2 + 1) * 128], HT.ap[:, ct, s * 512:(s + 1) * 512],
                               ct == 0, ct == 7, wg.k2(ct) + HT.k2(ct, s * 512, (s + 1) * 512), psk(gb[s]))
                        ev_g(s, gb[s])
                        for ct in range(8):
                            mm(PS[pbanks[s]][:, :], wp.ap[:, ct, f2 * 128:(f2 + 1) * 128], VY.ap[:, ct, s * 512:(s + 1) * 512],
                               ct == 0, ct == 7, wp.k2(ct) + VY.k2(ct, s * 512, (s + 1) * 512), psk(pbanks[s]))
                        t1 = sig[s]
                        mk_ = M.k2(n, s * 512, (s + 1) * 512)
                        if not accumulate:
                            tt(M.ap[:, n, s * 512:(s + 1) * 512], PS[pbanks[s]][:, :], t1.ap, ALU.mult,
                               psk(pbanks[s]) + t1.k(), mk_)
                        else:
                            tt(t1.ap, PS[pbanks[s]][:, :], t1.ap, ALU.mult, psk(pbanks[s]) + t1.k(), t1.k())
                            tt(M.ap[:, n, s * 512:(s + 1) * 512], M.ap[:, n, s * 512:(s + 1) * 512], t1.ap, ALU.add,
                               mk_ + t1.k(), mk_)
                w_release()
                w_release()

        proj_merge(False)

        wsl4 = [w_acquire() for _ in range(4)]
        P.op("dve", lambda e: e.memset(SSQ.ap, 0.0), writes=SSQ.k())
        VN = VY.ap.rearrange("p g (c f) -> p g c f", c=16)
        for i in range(16):
            gg = GG[i % 2]
            banks = (4, 5) if i % 2 == 0 else (6, 7)
            for j in range(4):
                bank = banks[j // 2]
                for ct in range(8):
                    mm(PS[bank][:, (j % 2) * 256:(j % 2 + 1) * 256], HT.ap[:, ct, i * 128:(i + 1) * 128],
                       wsl4[j].ap[:, ct, :], ct == 0, ct == 7,
                       HT.k2(ct, i * 128, (i + 1) * 128) + wsl4[j].k2(ct), psk(bank))
            for hb in range(2):
                act(gg.ap[:, hb * 512:(hb + 1) * 512], PS[banks[hb]][:, :], AF.Gelu_apprx_tanh, psk(banks[hb]),
                    gg.k(hb * 512, (hb + 1) * 512))
            junk = tmp([1024], BF16, 8704)
            act(junk.ap, gg.ap, AF.Square, gg.k(), junk.k() + SSQ.k(), accum_out=SSQ.ap[:, i:i + 1])
            wk = ()
            for g in range(8):
                wk += VY.k2(g, i * 128, (i + 1) * 128)
            tt(VN[:, :, i, :], gg.ap.rearrange("p (g f) -> p g f", g=8), GVB.ap.rearrange("p (g f) -> p g f", g=8),
               ALU.mult, gg.k() + GVB.k(), wk)
        P.op("dve", lambda e: e.tensor_scalar(out=SSQ.ap, in0=SSQ.ap, scalar1=1.0 / D, scalar2=EPS,
                                              op0=ALU.mult, op1=ALU.add), SSQ.k(), SSQ.k())
        act(SSQ.ap, SSQ.ap, AF.Ln, SSQ.k(), SSQ.k())
        act(SSQ.ap, SSQ.ap, AF.Exp, SSQ.k(), SSQ.k(), scale=-0.5)
        for c in range(16):
            wk = ()
            for g in range(8):
                wk += VY.k2(g, c * 128, (c + 1) * 128)
            P.op("dve", lambda e, c=c: e.tensor_scalar(out=VN[:, :, c, :], in0=VN[:, :, c, :], scalar1=SSQ.ap[:, c:c + 1],
                                                       scalar2=None, op0=ALU.mult), wk + SSQ.k(), wk)
        for _ in range(4):
            w_release()

        ub = [0, 1, 2, 3]
        gbk = [4, 5, 6, 7]
        a2w = {}
        sgt_i = [0]

        def a2_w(kind, g):
            key = (kind, g // 2)
            if key not in a2w:
                a2w[key] = w_acquire()
            return a2w[key]

        def a2_U(g):
            wu = a2_w("u", g)
            f2 = g % 2
            uu = UU2[g % 2]
            for s in range(4):
                for ct in range(8):
                    mm(PS[ub[s]][:, :], wu.ap[:, ct, f2 * 128:(f2 + 1) * 128], HT.ap[:, ct, s * 512:(s + 1) * 512],
                       ct == 0, ct == 7, wu.k2(ct) + HT.k2(ct, s * 512, (s + 1) * 512), psk(ub[s]))
                act(uu.ap[:, s * 512:(s + 1) * 512], PS[ub[s]][:, :], AF.Gelu_apprx_tanh, psk(ub[s]),
                    uu.k(s * 512, (s + 1) * 512))
            if f2 == 1:
                w_release()

        def a2_G(g):
            wgt = a2_w("g", g)
            f2 = g % 2
            uu = UU2[g % 2]
            for s in range(4):
                for ct in range(8):
                    mm(PS[gbk[s]][:, :], wgt.ap[:, ct, f2 * 128:(f2 + 1) * 128], HT.ap[:, ct, s * 512:(s + 1) * 512],
                       ct == 0, ct == 7, wgt.k2(ct) + HT.k2(ct, s * 512, (s + 1) * 512), psk(gbk[s]))
                sgt = SGT[sgt_i[0] % 2]
                sgt_i[0] += 1
                uus = uu.ap[:, s * 512:(s + 1) * 512]
                uuk = uu.k(s * 512, (s + 1) * 512)
                act(sgt.ap, PS[gbk[s]][:, :], AF.Tanh, psk(gbk[s]), sgt.k(), scale=0.5)
                P.op("dve", lambda e, sgt=sgt, s=s: e.scalar_tensor_tensor(
                    out=sgt.ap, in0=sgt.ap, scalar=1.0, in1=PS[gbk[s]][:, :], op0=ALU.add, op1=ALU.mult),
                    sgt.k() + psk(gbk[s]), sgt.k())
                P.op("dve", lambda e, sgt=sgt, uus=uus: e.scalar_tensor_tensor(
                    out=uus, in0=uus, scalar=0.5, in1=sgt.ap, op0=ALU.mult, op1=ALU.mult), uuk + sgt.k(), uuk)
            if f2 == 1:
                w_release()

        def a2_M(g):
            uu = UU2[g % 2]
            for s in range(4):
                brow = BSR.ap[0:1, g * 128:(g + 1) * 128].rearrange("p (o c) -> p o c", o=1).broadcast_to([1, 4, 128])
                mm(PS[gbk[s]][:, :].rearrange("p (a c) -> p a c", a=4), ONES.ap[0:1, :], brow, True, False,
                   ONES.k() + BSR.k(), psk(gbk[s]))
                for c4 in range(4):
                    c = s * 4 + c4
                    cols = slice(c4 * 128, (c4 + 1) * 128)
                    mm(PS[gbk[s]][:, cols], VN[:, g, c, :], WSM.ap[:, g, :], False, c4 == 3,
                       VY.k2(g, c * 128, (c + 1) * 128) + WSM.k2(g), psk(gbk[s]))
                tt(VY.ap[:, g, s * 512:(s + 1) * 512], PS[gbk[s]][:, :], uu.ap[:, s * 512:(s + 1) * 512], ALU.mult,
                   psk(gbk[s]) + uu.k(s * 512, (s + 1) * 512), VY.k2(g, s * 512, (s + 1) * 512))

        a2_U(0)
        a2_G(0)
        for g in range(8):
            if g + 1 < 8:
                a2_U(g + 1)
            a2_M(g)
            if g + 1 < 8:
                a2_G(g + 1)
        bankset[0] = 0

        proj_merge(True)

        wo4 = [w_acquire() for _ in range(4)]
        wb = 0
        for s in range(4):
            for m in range(8):
                j, f2 = m // 2, m % 2
                wo = wo4[j]
                bank = wb % 6
                wb += 1
                for n in range(8):
                    mm(PS[bank][:, :], wo.ap[:, n, f2 * 128:(f2 + 1) * 128], M.ap[:, n, s * 512:(s + 1) * 512],
                       n == 0, n == 7, wo.k2(n) + M.k2(n, s * 512, (s + 1) * 512), psk(bank))
                xk = XT.k2(m, s * 512, (s + 1) * 512)
                tt(XT.ap[:, m, s * 512:(s + 1) * 512], PS[bank][:, :], XT.ap[:, m, s * 512:(s + 1) * 512], ALU.add,
                   psk(bank) + xk, xk)
                if s == 3 and f2 == 1:
                    w_release()
            if l + 1 < l1:
                p0_span(l + 1, s, 6 + s % 2)
            elif final:
                p0_span(None, s, 6 + s % 2)

    OS = [Buf(A, [1024], F32, at=VY.off + i * 4096) for i in range(8)]
    for i in range(16):
        os_ = OS[i % 8]
        for half in range(2):
            bank = 4 + (i * 2 + half) % 4
            for c4 in range(4):
                ct = half * 4 + c4
                P.op("pe", lambda e, bank=bank, c4=c4, ct=ct, i=i: e.transpose(
                    PS[bank][:, c4 * 128:(c4 + 1) * 128], XT.ap[:, ct, i * 128:(i + 1) * 128], IDT.ap),
                    XT.k2(ct, i * 128, (i + 1) * 128) + IDT.k(), psk(bank))
            if half == 0:
                tcopy(os_.ap[:, 0:512], PS[bank][:, :], psk(bank), os_.k(0, 512))
            else:
                act(os_.ap[:, 512:1024], PS[bank][:, :], AF.Copy, psk(bank), os_.k(512, 1024))
        P.dma("sp", "os%d" % (i % 8), lambda e, os_=os_, i=i: e.dma_start(out=out_d[i * 128:(i + 1) * 128, :], in_=os_.ap),
              reads=os_.k(), writes=(("out", i),))
    P.emit(final_waits=tuple("os%d" % i for i in range(8)))
    return nc


_CONST = {}


def _consts():
    if _CONST:
        return _CONST
    k = np.arange(128)[:, None]
    q = np.arange(128)[None, :]
    prev = np.where(q <= k, 128 + q - k, BIG)
    cur = np.where(q >= k, q - k, BIG)
    _CONST["dist"] = np.concatenate([prev, cur], axis=1).astype(np.float32)
    _CONST["tri"] = (q >= k).astype(np.float32)
    _CONST["ident"] = np.eye(128, dtype=np.float32)
    return _CONST


_PROGS = {}


def _prog(l0, l1, final):
    key = (l0, l1, final)
    if key not in _PROGS:
        _PROGS[key] = build(l0, l1, final)
    return _PROGS[key]


FUSED = True


def kernel(x, g_norm, w_in, w_s, b_s, g_v, w_proj_a, w_proj_b, w_out, g_final):
    c = _consts()
    f32 = lambda a: np.ascontiguousarray(np.asarray(a, dtype=np.float32))
    x = f32(x)
    shared = {
        "w_in": f32(w_in), "w_pa": f32(w_proj_a), "w_pb": f32(w_proj_b), "w_o": f32(w_out),
        "w_s": f32(w_s), "bs": f32(b_s).reshape(NL, 1, 1024),
        "gvb": np.ascontiguousarray(np.broadcast_to(f32(g_v)[:, None, :], (NL, 128, 1024))),
        "gn": np.ascontiguousarray(f32(g_norm).reshape(NL, 8, 128).transpose(2, 0, 1).reshape(128, NL * 8)),
        "gf": np.ascontiguousarray(f32(g_final).reshape(8, 128).T),
        "ident": c["ident"], "tri": c["tri"], "dist": c["dist"],
    }
    n = x.shape[0]
    if FUSED:
        stages = [(0, NL, True)]
    else:
        stages = [(l, l + 1, l == NL - 1) for l in range(NL)]
    cur = [x[i] for i in range(n)]
    for (l0, l1, fin) in stages:
        nc = _prog(l0, l1, fin)
        in_maps = [dict(shared, x=np.ascontiguousarray(cur[i])) for i in range(n)]
        res = run_bass_kernel_spmd(nc, in_maps, core_ids=list(range(n)))
        cur = [np.asarray(res.results[i]["out"], dtype=np.float32) for i in range(n)]
    return np.stack(cur, axis=0)
```

```python
import numpy as np
import ml_dtypes
import concourse.bass as bass
import concourse.mybir as mybir
from concourse.bass_utils import run_bass_kernel_spmd

F32 = mybir.dt.float32
BF16 = mybir.dt.bfloat16
AF = mybir.ActivationFunctionType
ALU = mybir.AluOpType

S = 2048
D = 1024
NL = 4
NH = 16
EPS = 1e-6
BLK = 512
WCOLS = 256
NSLOT = 4
BIG = 1.0e5


class Prog:
    ENG = ("pe", "act", "dve", "pool", "sp")

    def __init__(self, nc):
        self.nc = nc
        self.ins = {e: [] for e in self.ENG}
        self.lw = {}
        self.rd = {}
        self.groups = {}
        self.known = {e: {} for e in self.ENG}

    def _collect(self, eng, reads, writes):
        deps = {}

        def add(t):
            src, val = t
            if deps.get(src, -1) < val:
                deps[src] = val

        for k in reads:
            t = self.lw.get(k)
            if t is not None:
                add(t)
        for k in writes:
            t = self.lw.get(k)
            if t is not None:
                add(t)
            r = self.rd.get(k)
            if r:
                for src, val in r.items():
                    add((src, val))
        waits = []
        kn = self.known[eng]
        for src, val in deps.items():
            if src == ("e", "pe") and eng == "pe":
                continue
            if kn.get(src, -1) >= val:
                continue
            kn[src] = val
            waits.append((src, val))
            if src[0] == "e":
                self.ins[src[1]][val]["signal"] = True
        return waits

    def _update(self, tok, reads, writes):
        src, val = tok
        for k in writes:
            self.lw[k] = tok
            self.rd[k] = {}
        for k in reads:
            r = self.rd.setdefault(k, {})
            if r.get(src, -1) < val:
                r[src] = val

    def op(self, eng, fn, reads=(), writes=()):
        waits = self._collect(eng, reads, writes)
        idx = len(self.ins[eng])
        self.ins[eng].append(dict(fn=fn, waits=waits, signal=False, dma=None))
        self._update((("e", eng), idx), reads, writes)

    def dma(self, queue, group, fn, reads=(), writes=()):
        gk = ("dg", group)
        waits = self._collect(queue, tuple(reads), tuple(writes) + (gk,))
        cnt = self.groups.get(group, 0) + 16
        self.groups[group] = cnt
        self.ins[queue].append(dict(fn=fn, waits=waits, signal=False, dma=group))
        self._update((("d", group), cnt), tuple(reads), tuple(writes) + (gk,))

    def emit(self, final_waits):
        nc = self.nc
        import contextlib
        with contextlib.ExitStack() as st:
            sems = {}
            for e in self.ENG:
                sems[("e", e)] = st.enter_context(nc.semaphore("s_" + e))
            for g in self.groups:
                sems[("d", g)] = st.enter_context(nc.semaphore("d_" + g))
            cnts = {}
            for e in self.ENG:
                c = 0
                arr = []
                for ins in self.ins[e]:
                    if ins["signal"]:
                        c += 1
                    arr.append(c)
                cnts[e] = arr
            block = st.enter_context(nc.Block())

            def run(e, eng):
                for ins in self.ins[e]:
                    for src, val in ins["waits"]:
                        v = cnts[src[1]][val] if src[0] == "e" else val
                        eng.wait_ge(sems[src], v)
                    r = ins["fn"](eng)
                    if ins["dma"] is not None:
                        r.then_inc(sems[("d", ins["dma"])], 16)
                    elif ins["signal"]:
                        r.then_inc(sems[("e", e)], 1)
                if e == "sp":
                    for g in final_waits:
                        eng.wait_ge(sems[("d", g)], self.groups[g])

            block.tensor(lambda eng: run("pe", eng))
            block.scalar(lambda eng: run("act", eng))
            block.vector(lambda eng: run("dve", eng))
            block.gpsimd(lambda eng: run("pool", eng))
            block.sync(lambda eng: run("sp", eng))


class Arena:
    def __init__(self, nc, nbytes):
        self.t = nc.alloc_sbuf_tensor("arena", [128, nbytes // 2], BF16)
        self.top = 0
        self.cap = nbytes

    def alloc(self, nbytes):
        off = self.top
        self.top += (nbytes + BLK - 1) // BLK * BLK
        assert self.top <= self.cap, (self.top, self.cap)
        return off


class Buf:
    def __init__(self, arena, shape, dt, at=None):
        self.es = 4 if dt == F32 else 2
        n = int(np.prod(shape))
        self.nbytes = n * self.es
        self.off = arena.alloc(self.nbytes) if at is None else at
        v = arena.t[:, self.off // 2:(self.off + self.nbytes) // 2]
        if dt == F32:
            v = v.bitcast(F32)
        if len(shape) == 2:
            v = v.rearrange("p (a b) -> p a b", a=shape[0])
        elif len(shape) == 3:
            v = v.rearrange("p (a b c) -> p a b c", a=shape[0], b=shape[1])
        self.ap = v
        self.shape = tuple(shape)

    def k(self, lo=0, hi=None):
        if hi is None:
            hi = self.nbytes // self.es
        b0 = (self.off + lo * self.es) // BLK
        b1 = (self.off + hi * self.es - 1) // BLK
        return tuple(("sb", b) for b in range(b0, b1 + 1))

    def k2(self, i, lo=0, hi=None):
        n = int(np.prod(self.shape[1:]))
        if hi is None:
            hi = n
        return self.k(i * n + lo, i * n + hi)


def build(l0, l1, final):
    nc = bass.Bass("TRN2", target_bir_lowering=False)
    dr = lambda name, shape, dt, kind: nc.dram_tensor(name, shape, dt, kind=kind).ap()
    x_d = dr("x", [S, D], F32, "ExternalInput")
    out_d = dr("out", [S, D], F32, "ExternalOutput")
    w_in = dr("w_in", [NL, D, 9216], F32, "ExternalInput")
    w_pa = dr("w_pa", [NL, D, D], F32, "ExternalInput")
    w_pb = dr("w_pb", [NL, D, D], F32, "ExternalInput")
    w_o = dr("w_o", [NL, D, D], F32, "ExternalInput")
    w_s = dr("w_s", [NL, 8, 128, 128], F32, "ExternalInput")
    bs_d = dr("bs", [NL, 1, 1024], F32, "ExternalInput")
    gvb_d = dr("gvb", [NL, 128, 1024], F32, "ExternalInput")
    gn_d = dr("gn", [128, NL * 8], F32, "ExternalInput")
    gf_d = dr("gf", [128, 8], F32, "ExternalInput")
    id_d = dr("ident", [128, 128], F32, "ExternalInput")
    tri_d = dr("tri", [128, 128], F32, "ExternalInput")
    dist_d = dr("dist", [128, 256], F32, "ExternalInput")
    vd = dr("vd", [S, 8, 192], BF16, "Internal")

    P = Prog(nc)
    A = Arena(nc, 212480)
    XT = Buf(A, [8, S], F32)
    HT = Buf(A, [8, S], BF16)
    VY = Buf(A, [8, S], BF16)
    r_off = A.alloc(34 * 1024)
    QT = [Buf(A, [S], BF16, at=r_off + i * 4096) for i in range(2)]
    KT = [Buf(A, [S], BF16, at=r_off + 8192 + i * 4096) for i in range(2)]
    VL0 = [Buf(A, [16, 192], BF16, at=r_off + 16384)] * 2
    VL1 = Buf(A, [16, 192], BF16, at=r_off + 16384 + 6144)
    VL2 = Buf(A, [16, 192], BF16, at=r_off + 16384 + 12288)

    def v_stat(vbuf, tile, hh):
        return vbuf.ap[:, tile, 0:128] if hh == 0 else vbuf.ap[:, tile, 64:192]
    M = Buf(A, [8, S], BF16, at=r_off)
    QT4 = Buf(A, [S], BF16)
    WS_ = [Buf(A, [8, WCOLS], BF16) for _ in range(NSLOT)]
    GN = Buf(A, [NL * 8], F32)
    GF = Buf(A, [8], F32)
    GVB = Buf(A, [1024], F32)
    WSM = Buf(A, [8, 128], BF16)
    BSR = Buf(A, [1024], BF16)
    IDT = Buf(A, [128], F32)
    TRI = Buf(A, [128], F32)
    DIST = Buf(A, [256], F32)
    ONES = Buf(A, [128], BF16)
    t_off = A.alloc(12 * 1024)
    tmp = lambda shape, dt, o: Buf(A, shape, dt, at=t_off + o)
    XS = [Buf(A, [1024], F32, at=VY.off + i * 4096) for i in range(8)]
    SQ = [tmp([512], BF16, i * 1024) for i in range(2)]
    RSTD = [tmp([512], F32, 2048 + i * 2048) for i in range(2)]
    VST = [tmp([4, 192], BF16, i * 1536) for i in range(6)]
    PT = [tmp([512], BF16, i * 1024) for i in range(4)] + [tmp([512], BF16, 10752)]
    DD = [tmp([512], F32, 4096 + i * 2048) for i in range(2)]
    MK = tmp([2, 2, 256], BF16, 8192)
    MK3 = tmp([2, 128], BF16, 8192 + 2048)
    T1 = [tmp([512], BF16, i * 1024) for i in range(4)]
    GG = [tmp([1024], F32, i * 4096) for i in range(2)]
    SSQ = tmp([16], F32, 8192)
    UU2 = [tmp([S], BF16, i * 4096) for i in range(2)]
    SGT = [tmp([512], BF16, 8704 + i * 1024) for i in range(2)]
    WST = tmp([8, 128], F32, 8192)
    WSC = [tmp([128], BF16, 10752 + i * 512) for i in range(3)]
    wsc_i = [0]
    PS = [nc.alloc_psum_tensor("ps%d" % i, [128, 512], F32) for i in range(8)]
    psk = lambda b: (("ps", b),)

    def mm(out, lhsT, rhs, start, stop, reads, writes):
        P.op("pe", lambda e: e.matmul(out, lhsT=lhsT, rhs=rhs, start=start, stop=stop,
                                      skip_group_check=True), reads, writes)

    def act(out, in_, func, reads, writes, scale=1.0, accum_out=None):
        if accum_out is None:
            P.op("act", lambda e: e.activation(out=out, in_=in_, func=func, scale=scale), reads, writes)
        else:
            P.op("act", lambda e: e.activation(out=out, in_=in_, func=func, scale=scale,
                                               accum_out=accum_out), reads, writes)

    def tt(out, in0, in1, op, reads, writes, eng="dve"):
        P.op(eng, lambda e: e.tensor_tensor(out=out, in0=in0, in1=in1, op=op), reads, writes)

    def tcopy(out, in_, reads, writes, eng="dve"):
        P.op(eng, lambda e: e.tensor_copy(out=out, in_=in_), reads, writes)

    chunks = []
    for l in range(l0, l1):
        cw = lambda base, j, l=l: w_in[l, :, base + j * WCOLS: base + (j + 1) * WCOLS]
        for j in range(4):
            chunks.append(cw(5120, j))
        for j in range(4):
            chunks.append(cw(3072, j))
            chunks.append(cw(4096, j))
        for j in range(4):
            chunks.append(cw(6144, j))
        for j in range(4):
            chunks.append(w_pb[l, :, j * WCOLS:(j + 1) * WCOLS])
            chunks.append(cw(8192, j))
        for j in range(4):
            chunks.append(cw(1024, j))
        for j in range(4):
            chunks.append(cw(0, j))
            chunks.append(cw(2048, j))
        for j in range(4):
            chunks.append(w_pa[l, :, j * WCOLS:(j + 1) * WCOLS])
            chunks.append(cw(7168, j))
        for j in range(4):
            chunks.append(w_o[l, :, j * WCOLS:(j + 1) * WCOLS])
    wstate = dict(next_load=0, next_use=0)

    def w_load():
        i = wstate["next_load"]
        if i >= len(chunks):
            return
        wstate["next_load"] = i + 1
        slot = WS_[i % NSLOT]
        src = chunks[i].rearrange("(kt p) f -> p kt f", p=128)
        P.dma("pool", "w%d" % (i % NSLOT), lambda e: e.dma_start(out=slot.ap, in_=src),
              reads=(), writes=slot.k())

    def w_acquire():
        i = wstate["next_use"]
        wstate["next_use"] = i + 1
        return WS_[i % NSLOT]

    def w_release():
        w_load()

    P.dma("sp", "c0", lambda e: e.dma_start(out=GN.ap, in_=gn_d), writes=GN.k())
    P.dma("sp", "c1", lambda e: e.dma_start(out=GF.ap, in_=gf_d), writes=GF.k())
    P.dma("sp", "c2", lambda e: e.dma_start(out=IDT.ap, in_=id_d), writes=IDT.k())
    P.dma("sp", "c3", lambda e: e.dma_start(out=TRI.ap, in_=tri_d), writes=TRI.k())
    P.dma("sp", "c4", lambda e: e.dma_start(out=DIST.ap, in_=dist_d), writes=DIST.k())
    P.op("dve", lambda e: e.memset(ONES.ap, 1.0), writes=ONES.k())
    for _ in range(NSLOT):
        w_load()

    def rmsnorm_stats(s, rs, bank=None):
        if bank is None:
            bank = s % 4
        for ct in range(8):
            sq = SQ[ct % 2]
            act(sq.ap, XT.ap[:, ct, s * 512:(s + 1) * 512], AF.Square,
                XT.k2(ct, s * 512, (s + 1) * 512), sq.k())
            mm(PS[bank][:, :], ONES.ap, sq.ap, ct == 0, ct == 7, ONES.k() + sq.k(), psk(bank))
        P.op("dve", lambda e: e.tensor_scalar(out=rs.ap, in0=PS[bank][:, :], scalar1=1.0 / D, scalar2=EPS,
                                              op0=ALU.mult, op1=ALU.add), psk(bank), rs.k())
        act(rs.ap, rs.ap, AF.Ln, rs.k(), rs.k())
        act(rs.ap, rs.ap, AF.Exp, rs.k(), rs.k(), scale=-0.5)

    def proj_fm(wslot, c0, banks, evac):
        for s in range(4):
            for ct in range(8):
                mm(PS[banks[s]][:, :], wslot.ap[:, ct, c0:c0 + 128], HT.ap[:, ct, s * 512:(s + 1) * 512],
                   ct == 0, ct == 7, wslot.k2(ct) + HT.k2(ct, s * 512, (s + 1) * 512), psk(banks[s]))
            evac(s, banks[s])

    def p0_span(l_next, s, ssbank):
        rs = RSTD[s % 2]
        rmsnorm_stats(s, rs, ssbank)
        for ct in range(8):
            xk = XT.k2(ct, s * 512, (s + 1) * 512)
            if l_next is not None:
                P.op("dve", lambda e, ct=ct, rs=rs: e.scalar_tensor_tensor(
                    out=HT.ap[:, ct, s * 512:(s + 1) * 512], in0=XT.ap[:, ct, s * 512:(s + 1) * 512],
                    scalar=GN.ap[:, (l_next * 8 + ct):(l_next * 8 + ct) + 1], in1=rs.ap, op0=ALU.mult, op1=ALU.mult),
                    xk + GN.k() + rs.k(), HT.k2(ct, s * 512, (s + 1) * 512))
            else:
                P.op("dve", lambda e, ct=ct, rs=rs: e.scalar_tensor_tensor(
                    out=XT.ap[:, ct, s * 512:(s + 1) * 512], in0=XT.ap[:, ct, s * 512:(s + 1) * 512],
                    scalar=GF.ap[:, ct:ct + 1], in1=rs.ap, op0=ALU.mult, op1=ALU.mult), xk + GF.k() + rs.k(), xk)

    for i in range(16):
        xs = XS[i % 8]
        P.dma("sp", "xs%d" % (i % 8), lambda e, xs=xs, i=i: e.dma_start(out=xs.ap, in_=x_d[i * 128:(i + 1) * 128, :]),
              writes=xs.k())
        for half in range(2):
            bank = (i * 2 + half) % 4
            for c4 in range(4):
                ct = half * 4 + c4
                P.op("pe", lambda e, bank=bank, c4=c4, xs=xs, ct=ct: e.transpose(
                    PS[bank][:, c4 * 128:(c4 + 1) * 128], xs.ap[:, ct * 128:(ct + 1) * 128], IDT.ap),
                    reads=xs.k(ct * 128, (ct + 1) * 128) + IDT.k(), writes=psk(bank))
            dst = XT.ap[:, half * 4:half * 4 + 4, i * 128:(i + 1) * 128]
            src = PS[bank][:, :].rearrange("p (a b) -> p a b", a=4)
            wk = ()
            for c4 in range(4):
                wk += XT.k2(half * 4 + c4, i * 128, (i + 1) * 128)
            if half == 0:
                tcopy(dst, src, psk(bank), wk, eng="dve")
            else:
                act(dst, src, AF.Copy, psk(bank), wk)
        if i % 4 == 3:
            p0_span(l0, i // 4, 4 + (i // 4) % 4)

    slopes = [2.0 ** (-8.0 * (h + 1) / NH) for h in range(NH)]
    bankset = [0]

    def next_banks():
        b = bankset[0]
        bankset[0] = 4 - b
        return [b, b + 1, b + 2, b + 3]

    for l in range(l0, l1):
        P.dma("sp", "lp0", lambda e, l=l: e.dma_start(out=GVB.ap, in_=gvb_d[l]), writes=GVB.k())
        P.dma("pool", "lpb", lambda e, l=l: e.dma_start(out=BSR.ap[0:1, :], in_=bs_d[l]), writes=BSR.k())
        P.dma("sp", "lp1", lambda e, l=l: e.dma_start(out=WST.ap, in_=w_s[l].rearrange("g t s -> t g s")),
              writes=WST.k())
        for g in range(8):
            bank = 4 + (g % 4)
            P.op("pe", lambda e, g=g, bank=bank: e.transpose(PS[bank][:, 0:128], WST.ap[:, g, :], IDT.ap),
                 WST.k() + IDT.k(), psk(bank))
            tt(WSM.ap[:, g, :], PS[bank][:, 0:128], TRI.ap, ALU.mult, psk(bank) + TRI.k(), WSM.k2(g))

        for half in range(2):
            wv2 = [w_acquire(), w_acquire()]
            for i in range(16):
                bank = 4 + i % 4
                vst = VST[i % 6]
                if half == 0 and i < 6:
                    P.op("dve", lambda e, vst=vst: e.memset(vst.ap[:, :, 64:128], 1.0), writes=vst.k())
                for j in range(2):
                    for ct in range(8):
                        mm(PS[bank][:, j * 256:(j + 1) * 256], HT.ap[:, ct, i * 128:(i + 1) * 128],
                           wv2[j].ap[:, ct, :], ct == 0, ct == 7,
                           HT.k2(ct, i * 128, (i + 1) * 128) + wv2[j].k2(ct), psk(bank))
                src = PS[bank][:, :].rearrange("p (a h c) -> p a h c", a=4, h=2)
                dst = vst.ap.rearrange("p a (h c) -> p a h c", h=3)[:, :, 0:3:2, :]
                if i % 2 == 0:
                    tcopy(dst, src, psk(bank), vst.k())
                else:
                    act(dst, src, AF.Copy, psk(bank), vst.k())
                P.dma("sp", "vst%d" % (i % 6), lambda e, vst=vst, i=i, half=half: e.dma_start(
                    out=vd[i * 128:(i + 1) * 128, half * 4:(half + 1) * 4, :], in_=vst.ap),
                    reads=vst.k(), writes=tuple(("vd", hp_, i) for hp_ in range(half * 4, half * 4 + 4)))
            w_release()
            w_release()

        def load_v0(hp):
            rk = tuple(("vd", hp, i_) for i_ in range(16))
            vb = VL0[hp % 2]
            P.dma("sp", "vl0", lambda e: e.dma_start(
                out=vb.ap, in_=vd[:, hp, :].rearrange("(i p) c -> p i c", p=128)), reads=rk, writes=vb.k())

        def load_v1(hp):
            rk = tuple(("vd", hp, i_) for i_ in range(16))
            P.dma("sp", "vl1", lambda e: e.dma_start(
                out=VL1.ap.rearrange("p (r b) c -> p r b c", r=4),
                in_=vd[:, hp, :].rearrange("(b p r) c -> p r b c", b=4, p=128, r=4)), reads=rk, writes=VL1.k())

        def load_v2(hp):
            rk = tuple(("vd", hp, i_) for i_ in range(16))
            P.dma("sp", "vl2", lambda e: e.dma_start(
                out=VL2.ap, in_=vd[:, hp, :].rearrange("(p r) c -> p r c", r=16)), reads=rk, writes=VL2.k())

        qk_slots = {}

        def qk_item(hp, which, s, qb=7):
            j = hp // 2
            if (hp % 2 == 0) and which == 0 and s == 0:
                qk_slots[j] = (w_acquire(), w_acquire())
            wsl = qk_slots[j][which]
            c0 = (hp % 2) * 128
            dst = (QT if which == 0 else KT)[hp % 2]
            for ct in range(8):
                mm(PS[qb][:, :], wsl.ap[:, ct, c0:c0 + 128], HT.ap[:, ct, s * 512:(s + 1) * 512],
                   ct == 0, ct == 7, wsl.k2(ct) + HT.k2(ct, s * 512, (s + 1) * 512), psk(qb))
            tcopy(dst.ap[:, s * 512:(s + 1) * 512], PS[qb][:, :], psk(qb), dst.k(s * 512, (s + 1) * 512))
            if (hp % 2 == 1) and which == 1 and s == 3:
                w_release()
                w_release()

        items = [(w, s) for s in range(4) for w in range(2)]
        for ii, (w, s) in enumerate(items):
            qk_item(0, w, s, ii)

        load_v0(0)
        load_v2(0)
        load_v1(0)
        tasks = []
        first_flag = {}
        for hp in range(8):
            qt, kt = QT[hp % 2], KT[hp % 2]
            vl0 = VL0[hp % 2]

            def gen_masks(hp=hp, qt=qt):
                P.op("dve", lambda e, qt=qt: e.tensor_copy(
                    out=QT4.ap.rearrange("p (b r j) -> p b r j", b=4, r=4),
                    in_=qt.ap.rearrange("p (b j r) -> p b r j", b=4, r=4)), qt.k(), QT4.k())
                for hh in range(2):
                    sl = slopes[2 * hp + hh]
                    for pi, dil in enumerate((1, 4)):
                        act(MK.ap[:, hh, pi, :], DIST.ap, AF.Exp, DIST.k(), MK.k(), scale=-sl * dil)
                    act(MK3.ap[:, hh, :], DIST.ap[:, 128:256], AF.Exp, DIST.k(), MK3.k(), scale=-sl * 16)
            ptasks = []
            for b in range(4):
                obank = {0: (b % 2) * 2, 1: (b % 2) * 2 + 1}
                for hh in range(2):
                    first_flag[(hp, b, hh)] = True

                def pv(hh, vbuf, tile, ptile, pcols, ocols, obank=obank, hp=hp, b=b):
                    ob = obank[hh]
                    fkey = (hp, b, hh)
                    mm(PS[ob][:, ocols], v_stat(vbuf, tile, hh), ptile.ap[:, pcols],
                       first_flag[fkey], False, vbuf.k() + ptile.k(), psk(ob))
                    first_flag[fkey] = False

                work = []
                for jq in range(4):
                    j = 4 * b + jq
                    qsl = slice(j * 128, (j + 1) * 128)
                    ksl_prev = slice((j - 1) * 128, j * 128) if j > 0 else None
                    work.append((0, qsl, ksl_prev, qsl, (vl0, j - 1), (vl0, j), slice(jq * 128, (jq + 1) * 128)))
                for r in range(4):
                    qsl = slice(512 * b + r, 512 * (b + 1), 4)
                    ksl_prev = slice(512 * (b - 1) + r, 512 * b, 4) if b > 0 else None
                    work.append((1, slice(512 * b + 128 * r, 512 * b + 128 * (r + 1)), ksl_prev, qsl,
                                 (VL1, r * 4 + b - 1), (VL1, r * 4 + b), slice(r, 512, 4)))
                for wi in range(0, 8, 2):
                    for hh in range(2):
                        def s_fn(tk, wi=wi, work=work, b=b, qt=qt, kt=kt, hh=hh):
                            sb, ptile = tk["sb"], tk["pt"]
                            pb = hh * 64
                            pat = work[wi][0]
                            for u in range(2):
                                _, qsl, kprev, kcur, vprev, vcur, ocols = work[wi + u]
                                qsrc = QT4 if pat == 1 else qt
                                qrd = qsrc.k(qsl.start, qsl.stop)
                                if kprev is not None:
                                    mm(PS[sb][:, u * 256:u * 256 + 128], kt.ap[pb:pb + 64, kprev], qsrc.ap[pb:pb + 64, qsl],
                                       True, True, kt.k(max(0, 512 * (b - 1)), 512 * (b + 1)) + qrd, psk(sb))
                                mm(PS[sb][:, u * 256 + 128:u * 256 + 256], kt.ap[pb:pb + 64, kcur], qsrc.ap[pb:pb + 64, qsl],
                                   True, True, kt.k(512 * b, 512 * (b + 1)) + qrd, psk(sb))
                            act(ptile.ap, PS[sb][:, :], AF.Exp, psk(sb), ptile.k(), scale=0.125)
                            mk = MK.ap[:, hh, pat, :].rearrange("p (o c) -> p o c", o=1).broadcast_to([128, 2, 256])
                            p3 = ptile.ap.rearrange("p (a c) -> p a c", a=2)
                            tt(p3, p3, mk, ALU.mult, ptile.k() + MK.k(), ptile.k())

                        def pv_fn(tk, wi=wi, work=work, pv=pv, hh=hh):
                            ptile = tk["pt"]
                            for u in range(2):
                                _, qsl, kprev, kcur, vprev, vcur, ocols = work[wi + u]
                                if kprev is not None:
                                    pv(hh, vprev[0], vprev[1], ptile, slice(u * 256, u * 256 + 128), ocols)
                                pv(hh, vcur[0], vcur[1], ptile, slice(u * 256 + 128, u * 256 + 256), ocols)
                        tkw = dict(s=s_fn, pv=pv_fn, before=[], after=[])
                        ptasks.append(tkw)
                        if b == 3 and wi == 2 and hh == 1 and hp + 1 < 8:
                            tkw["after"].append(lambda hp=hp: load_v0(hp + 1))

                for hh in range(2):
                    def s3_fn(tk, hh=hh, b=b, qt=qt, kt=kt):
                        sb, ptile = tk["sb"], tk["pt"]
                        pb = hh * 64
                        for r in range(16):
                            mm(PS[sb][:, r * 32:(r + 1) * 32], kt.ap[pb:pb + 64, r:S:16],
                               qt.ap[pb:pb + 64, 512 * b + r:512 * (b + 1):16], True, True,
                               kt.k() + qt.k(512 * b, 512 * (b + 1)), psk(sb))
                        act(ptile.ap, PS[sb][:, :], AF.Exp, psk(sb), ptile.k(), scale=0.125)
                        mk = MK3.ap[:, hh, 32 * b:32 * (b + 1)].rearrange("p (o c) -> p o c", o=1).broadcast_to([128, 16, 32])
                        p3 = ptile.ap.rearrange("p (a c) -> p a c", a=16)
                        tt(p3, p3, mk, ALU.mult, ptile.k() + MK3.k(), ptile.k())

                    def pv3_fn(tk, hh=hh, pv=pv):
                        ptile = tk["pt"]
                        for r in range(16):
                            pv(hh, VL2, r, ptile, slice(r * 32, (r + 1) * 32), slice(r, 512, 16))
                    tk3 = dict(s=s3_fn, pv=pv3_fn, before=[], after=[])
                    ptasks.insert(len(ptasks) - (4 if (hh == 0 or b == 3) else 0), tk3)
                    if b == 3 and hh == 1 and hp + 1 < 8:
                        tk3["after"].append(lambda hp=hp: load_v2(hp + 1))

                def norm_fn(b=b, obank=obank, hp=hp):
                    dd = DD[b % 2]
                    oa, obb = obank[0], obank[1]
                    act(dd.ap[0:64, :], PS[oa][64:128, :], AF.Ln, psk(oa), dd.k())
                    act(dd.ap[64:128, :], PS[obb][0:64, :], AF.Ln, psk(obb), dd.k())
                    act(dd.ap, dd.ap, AF.Exp, dd.k(), dd.k(), scale=-1.0)
                    tt(VY.ap[0:64, hp, b * 512:(b + 1) * 512], PS[oa][0:64, :], dd.ap[0:64, :], ALU.mult,
                       psk(oa) + dd.k(), VY.k2(hp, b * 512, (b + 1) * 512))
                    tt(VY.ap[64:128, hp, b * 512:(b + 1) * 512], PS[obb][64:128, :], dd.ap[64:128, :], ALU.mult,
                       psk(obb) + dd.k(), VY.k2(hp, b * 512, (b + 1) * 512))
                ptasks[-1]["after"].append(norm_fn)
            ptasks[0]["before"].append(gen_masks)
            if hp + 1 < 8:
                for ii, (w_, s_) in reversed(list(enumerate(items))):
                    ptasks.insert(5 * ii + 5, dict(
                        s=lambda tk, hp=hp, w_=w_, s_=s_: qk_item(hp + 1, w_, s_, tk["sb"]),
                        pv=lambda tk: None, before=[], after=[]))
                ptasks[-1]["after"].append(lambda hp=hp: load_v1(hp + 1))
            tasks += ptasks

        SKEW = 3
        for t in range(len(tasks) + SKEW):
            if t < len(tasks):
                tk = tasks[t]
                tk["sb"] = 4 + t % 4
                tk["pt"] = PT[t % 5]
                for f in tk["before"]:
                    f()
                tk["s"](tk)
            if t >= SKEW:
                tk = tasks[t - SKEW]
                tk["pv"](tk)
                for f in tk["after"]:
                    f()

        ti = [0]
        for j in range(4):
            wsl = w_acquire()
            for f2 in range(2):
                hp = 2 * j + f2

                def ev(s, bank, hp=hp):
                    t1 = T1[ti[0] % 4]
                    ti[0] += 1
                    act(t1.ap, PS[bank][:, :], AF.Silu, psk(bank), t1.k())
                    tt(VY.ap[:, hp, s * 512:(s + 1) * 512], VY.ap[:, hp, s * 512:(s + 1) * 512], t1.ap, ALU.mult,
                       VY.k2(hp, s * 512, (s + 1) * 512) + t1.k(), VY.k2(hp, s * 512, (s + 1) * 512))
                proj_fm(wsl, f2 * 128, next_banks(), ev)
            w_release()

        def proj_merge(accumulate):
            for j in range(4):
                wp = w_acquire()
                wg = w_acquire()
                for f2 in range(2):
                    n = 2 * j + f2
                    gb = next_banks()
                    sig = []

                    def ev_g(s, bank):
                        t1 = T1[ti[0] % 4]
                        ti[0] += 1
                        act(t1.ap, PS[bank][:, :], AF.Sigmoid, psk(bank), t1.k())
                        sig.append(t1)
                    pbanks = next_banks()
                    for s in range(4):
                        for ct in range(8):
                            mm(PS[gb[s]][:, :], wg.ap[:, ct, f2 * 128:(f2 + 1) * 128], HT.ap[:, ct, s * 512:(s + 1) * 512],
                               ct == 0, ct == 7, wg.k2(ct) + HT.k2(ct, s * 512, (s + 1) * 512), psk(gb[s]))
                        ev_g(s, gb[s])
                        for ct in range(8):
                            mm(PS[pbanks[s]][:, :], wp.ap[:, ct, f2 * 128:(f2 + 1) * 128], VY.ap[:, ct, s * 512:(s + 1) * 512],
                               ct == 0, ct == 7, wp.k2(ct) + VY.k2(ct, s * 512, (s + 1) * 512), psk(pbanks[s]))
                        t1 = sig[s]
                        mk_ = M.k2(n, s * 512, (s + 1) * 512)
                        if not accumulate:
                            tt(M.ap[:, n, s * 512:(s + 1) * 512], PS[pbanks[s]][:, :], t1.ap, ALU.mult,
                               psk(pbanks[s]) + t1.k(), mk_)
                        else:
                            tt(t1.ap, PS[pbanks[s]][:, :], t1.ap, ALU.mult, psk(pbanks[s]) + t1.k(), t1.k())
                            tt(M.ap[:, n, s * 512:(s + 1) * 512], M.ap[:, n, s * 512:(s + 1) * 512], t1.ap, ALU.add,
                               mk_ + t1.k(), mk_)
                w_release()
                w_release()

        proj_merge(False)

        wsl4 = [w_acquire() for _ in range(4)]
        P.op("dve", lambda e: e.memset(SSQ.ap, 0.0), writes=SSQ.k())
        VN = VY.ap.rearrange("p g (c f) -> p g c f", c=16)
        for i in range(16):
            gg = GG[i % 2]
            banks = (4, 5) if i % 2 == 0 else (6, 7)
            for j in range(4):
                bank = banks[j // 2]
                for ct in range(8):
                    mm(PS[bank][:, (j % 2) * 256:(j % 2 + 1) * 256], HT.ap[:, ct, i * 128:(i + 1) * 128],
                       wsl4[j].ap[:, ct, :], ct == 0, ct == 7,
                       HT.k2(ct, i * 128, (i + 1) * 128) + wsl4[j].k2(ct), psk(bank))
            for hb in range(2):
                act(gg.ap[:, hb * 512:(hb + 1) * 512], PS[banks[hb]][:, :], AF.Gelu_apprx_tanh, psk(banks[hb]),
                    gg.k(hb * 512, (hb + 1) * 512))
            junk = tmp([1024], BF16, 8704)
            act(junk.ap, gg.ap, AF.Square, gg.k(), junk.k() + SSQ.k(), accum_out=SSQ.ap[:, i:i + 1])
            wk = ()
            for g in range(8):
                wk += VY.k2(g, i * 128, (i + 1) * 128)
            tt(VN[:, :, i, :], gg.ap.rearrange("p (g f) -> p g f", g=8), GVB.ap.rearrange("p (g f) -> p g f", g=8),
               ALU.mult, gg.k() + GVB.k(), wk)
        P.op("dve", lambda e: e.tensor_scalar(out=SSQ.ap, in0=SSQ.ap, scalar1=1.0 / D, scalar2=EPS,
                                              op0=ALU.mult, op1=ALU.add), SSQ.k(), SSQ.k())
        act(SSQ.ap, SSQ.ap, AF.Ln, SSQ.k(), SSQ.k())
        act(SSQ.ap, SSQ.ap, AF.Exp, SSQ.k(), SSQ.k(), scale=-0.5)
        for c in range(16):
            wk = ()
            for g in range(8):
                wk += VY.k2(g, c * 128, (c + 1) * 128)
            P.op("dve", lambda e, c=c: e.tensor_scalar(out=VN[:, :, c, :], in0=VN[:, :, c, :], scalar1=SSQ.ap[:, c:c + 1],
                                                       scalar2=None, op0=ALU.mult), wk + SSQ.k(), wk)
        for _ in range(4):
            w_release()

        ub = [0, 1, 2, 3]
        gbk = [4, 5, 6, 7]
        a2w = {}
        sgt_i = [0]

        def a2_w(kind, g):
            key = (kind, g // 2)
            if key not in a2w:
                a2w[key] = w_acquire()
            return a2w[key]

        def a2_U(g):
            wu = a2_w("u", g)
            f2 = g % 2
            uu = UU2[g % 2]
            for s in range(4):
                for ct in range(8):
                    mm(PS[ub[s]][:, :], wu.ap[:, ct, f2 * 128:(f2 + 1) * 128], HT.ap[:, ct, s * 512:(s + 1) * 512],
                       ct == 0, ct == 7, wu.k2(ct) + HT.k2(ct, s * 512, (s + 1) * 512), psk(ub[s]))
                act(uu.ap[:, s * 512:(s + 1) * 512], PS[ub[s]][:, :], AF.Gelu_apprx_tanh, psk(ub[s]),
                    uu.k(s * 512, (s + 1) * 512))
            if f2 == 1:
                w_release()

        def a2_G(g):
            wgt = a2_w("g", g)
            f2 = g % 2
            uu = UU2[g % 2]
            for s in range(4):
                for ct in range(8):
                    mm(PS[gbk[s]][:, :], wgt.ap[:, ct, f2 * 128:(f2 + 1) * 128], HT.ap[:, ct, s * 512:(s + 1) * 512],
                       ct == 0, ct == 7, wgt.k2(ct) + HT.k2(ct, s * 512, (s + 1) * 512), psk(gbk[s]))
                sgt = SGT[sgt_i[0] % 2]
                sgt_i[0] += 1
                uus = uu.ap[:, s * 512:(s + 1) * 512]
                uuk = uu.k(s * 512, (s + 1) * 512)
                act(sgt.ap, PS[gbk[s]][:, :], AF.Tanh, psk(gbk[s]), sgt.k(), scale=0.5)
                P.op("dve", lambda e, sgt=sgt, s=s: e.scalar_tensor_tensor(
                    out=sgt.ap, in0=sgt.ap, scalar=1.0, in1=PS[gbk[s]][:, :], op0=ALU.add, op1=ALU.mult),
                    sgt.k() + psk(gbk[s]), sgt.k())
                P.op("dve", lambda e, sgt=sgt, uus=uus: e.scalar_tensor_tensor(
                    out=uus, in0=uus, scalar=0.5, in1=sgt.ap, op0=ALU.mult, op1=ALU.mult), uuk + sgt.k(), uuk)
            if f2 == 1:
                w_release()

        def a2_M(g):
            uu = UU2[g % 2]
            for s in range(4):
                brow = BSR.ap[0:1, g * 128:(g + 1) * 128].rearrange("p (o c) -> p o c", o=1).broadcast_to([1, 4, 128])
                mm(PS[gbk[s]][:, :].rearrange("p (a c) -> p a c", a=4), ONES.ap[0:1, :], brow, True, False,
                   ONES.k() + BSR.k(), psk(gbk[s]))
                for c4 in range(4):
                    c = s * 4 + c4
                    cols = slice(c4 * 128, (c4 + 1) * 128)
                    mm(PS[gbk[s]][:, cols], VN[:, g, c, :], WSM.ap[:, g, :], False, c4 == 3,
                       VY.k2(g, c * 128, (c + 1) * 128) + WSM.k2(g), psk(gbk[s]))
                tt(VY.ap[:, g, s * 512:(s + 1) * 512], PS[gbk[s]][:, :], uu.ap[:, s * 512:(s + 1) * 512], ALU.mult,
                   psk(gbk[s]) + uu.k(s * 512, (s + 1) * 512), VY.k2(g, s * 512, (s + 1) * 512))

        a2_U(0)
        a2_G(0)
        for g in range(8):
            if g + 1 < 8:
                a2_U(g + 1)
            a2_M(g)
            if g + 1 < 8:
                a2_G(g + 1)
        bankset[0] = 0

        proj_merge(True)

        wo4 = [w_acquire() for _ in range(4)]
        wb = 0
        for s in range(4):
            for m in range(8):
                j, f2 = m // 2, m % 2
                wo = wo4[j]
                bank = wb % 6
                wb += 1
                for n in range(8):
                    mm(PS[bank][:, :], wo.ap[:, n, f2 * 128:(f2 + 1) * 128], M.ap[:, n, s * 512:(s + 1) * 512],
                       n == 0, n == 7, wo.k2(n) + M.k2(n, s * 512, (s + 1) * 512), psk(bank))
                xk = XT.k2(m, s * 512, (s + 1) * 512)
                tt(XT.ap[:, m, s * 512:(s + 1) * 512], PS[bank][:, :], XT.ap[:, m, s * 512:(s + 1) * 512], ALU.add,
                   psk(bank) + xk, xk)
                if s == 3 and f2 == 1:
                    w_release()
            if l + 1 < l1:
                p0_span(l + 1, s, 6 + s % 2)
            elif final:
                p0_span(None, s, 6 + s % 2)

    OS = [Buf(A, [1024], F32, at=VY.off + i * 4096) for i in range(8)]
    for i in range(16):
        os_ = OS[i % 8]
        for half in range(2):
            bank = 4 + (i * 2 + half) % 4
            for c4 in range(4):
                ct = half * 4 + c4
                P.op("pe", lambda e, bank=bank, c4=c4, ct=ct, i=i: e.transpose(
                    PS[bank][:, c4 * 128:(c4 + 1) * 128], XT.ap[:, ct, i * 128:(i + 1) * 128], IDT.ap),
                    XT.k2(ct, i * 128, (i + 1) * 128) + IDT.k(), psk(bank))
            if half == 0:
                tcopy(os_.ap[:, 0:512], PS[bank][:, :], psk(bank), os_.k(0, 512))
            else:
                act(os_.ap[:, 512:1024], PS[bank][:, :], AF.Copy, psk(bank), os_.k(512, 1024))
        P.dma("sp", "os%d" % (i % 8), lambda e, os_=os_, i=i: e.dma_start(out=out_d[i * 128:(i + 1) * 128, :], in_=os_.ap),
              reads=os_.k(), writes=(("out", i),))
    P.emit(final_waits=tuple("os%d" % i for i in range(8)))
    return nc


_CONST = {}


def _consts():
    if _CONST:
        return _CONST
    k = np.arange(128)[:, None]
    q = np.arange(128)[None, :]
    prev = np.where(q <= k, 128 + q - k, BIG)
    cur = np.where(q >= k, q - k, BIG)
    _CONST["dist"] = np.concatenate([prev, cur], axis=1).astype(np.float32)
    _CONST["tri"] = (q >= k).astype(np.float32)
    _CONST["ident"] = np.eye(128, dtype=np.float32)
    return _CONST


_PROGS = {}


def _prog(l0, l1, final):
    key = (l0, l1, final)
    if key not in _PROGS:
        _PROGS[key] = build(l0, l1, final)
    return _PROGS[key]


FUSED = True


def kernel(x, g_norm, w_in, w_s, b_s, g_v, w_proj_a, w_proj_b, w_out, g_final):
    c = _consts()
    f32 = lambda a: np.ascontiguousarray(np.asarray(a, dtype=np.float32))
    x = f32(x)
    shared = {
        "w_in": f32(w_in), "w_pa": f32(w_proj_a), "w_pb": f32(w_proj_b), "w_o": f32(w_out),
        "w_s": f32(w_s), "bs": f32(b_s).reshape(NL, 1, 1024),
        "gvb": np.ascontiguousarray(np.broadcast_to(f32(g_v)[:, None, :], (NL, 128, 1024))),
        "gn": np.ascontiguousarray(f32(g_norm).reshape(NL, 8, 128).transpose(2, 0, 1).reshape(128, NL * 8)),
        "gf": np.ascontiguousarray(f32(g_final).reshape(8, 128).T),
        "ident": c["ident"], "tri": c["tri"], "dist": c["dist"],
    }
    n = x.shape[0]
    if FUSED:
        stages = [(0, NL, True)]
    else:
        stages = [(l, l + 1, l == NL - 1) for l in range(NL)]
    cur = [x[i] for i in range(n)]
    for (l0, l1, fin) in stages:
        nc = _prog(l0, l1, fin)
        in_maps = [dict(shared, x=np.ascontiguousarray(cur[i])) for i in range(n)]
        res = run_bass_kernel_spmd(nc, in_maps, core_ids=list(range(n)))
        cur = [np.asarray(res.results[i]["out"], dtype=np.float32) for i in range(n)]
    return np.stack(cur, axis=0)
```

```python
import numpy as np
import ml_dtypes
import concourse.bass as bass
import concourse.mybir as mybir
from concourse.bass_utils import run_bass_kernel_spmd

F32 = mybir.dt.float32
BF16 = mybir.dt.bfloat16
AF = mybir.ActivationFunctionType
ALU = mybir.AluOpType

S = 2048
D = 1024
NL = 4
NH = 16
EPS = 1e-6
BLK = 512
WCOLS = 256
NSLOT = 4
BIG = 1.0e5


class Prog:
    ENG = ("pe", "act", "dve", "pool", "sp")

    def __init__(self, nc):
        self.nc = nc
        self.ins = {e: [] for e in self.ENG}
        self.lw = {}
        self.rd = {}
        self.groups = {}
        self.known = {e: {} for e in self.ENG}

    def _collect(self, eng, reads, writes):
        deps = {}

        def add(t):
            src, val = t
            if deps.get(src, -1) < val:
                deps[src] = val

        for k in reads:
            t = self.lw.get(k)
            if t is not None:
                add(t)
        for k in writes:
            t = self.lw.get(k)
            if t is not None:
                add(t)
            r = self.rd.get(k)
            if r:
                for src, val in r.items():
                    add((src, val))
        waits = []
        kn = self.known[eng]
        for src, val in deps.items():
            if src == ("e", "pe") and eng == "pe":
                continue
            if kn.get(src, -1) >= val:
                continue
            kn[src] = val
            waits.append((src, val))
            if src[0] == "e":
                self.ins[src[1]][val]["signal"] = True
        return waits

    def _update(self, tok, reads, writes):
        src, val = tok
        for k in writes:
            self.lw[k] = tok
            self.rd[k] = {}
        for k in reads:
            r = self.rd.setdefault(k, {})
            if r.get(src, -1) < val:
                r[src] = val

    def op(self, eng, fn, reads=(), writes=()):
        waits = self._collect(eng, reads, writes)
        idx = len(self.ins[eng])
        self.ins[eng].append(dict(fn=fn, waits=waits, signal=False, dma=None))
        self._update((("e", eng), idx), reads, writes)

    def dma(self, queue, group, fn, reads=(), writes=()):
        gk = ("dg", group)
        waits = self._collect(queue, tuple(reads), tuple(writes) + (gk,))
        cnt = self.groups.get(group, 0) + 16
        self.groups[group] = cnt
        self.ins[queue].append(dict(fn=fn, waits=waits, signal=False, dma=group))
        self._update((("d", group), cnt), tuple(reads), tuple(writes) + (gk,))

    def emit(self, final_waits):
        nc = self.nc
        import contextlib
        with contextlib.ExitStack() as st:
            sems = {}
            for e in self.ENG:
                sems[("e", e)] = st.enter_context(nc.semaphore("s_" + e))
            for g in self.groups:
                sems[("d", g)] = st.enter_context(nc.semaphore("d_" + g))
            cnts = {}
            for e in self.ENG:
                c = 0
                arr = []
                for ins in self.ins[e]:
                    if ins["signal"]:
                        c += 1
                    arr.append(c)
                cnts[e] = arr
            block = st.enter_context(nc.Block())

            def run(e, eng):
                for ins in self.ins[e]:
                    for src, val in ins["waits"]:
                        v = cnts[src[1]][val] if src[0] == "e" else val
                        eng.wait_ge(sems[src], v)
                    r = ins["fn"](eng)
                    if ins["dma"] is not None:
                        r.then_inc(sems[("d", ins["dma"])], 16)
                    elif ins["signal"]:
                        r.then_inc(sems[("e", e)], 1)
                if e == "sp":
                    for g in final_waits:
                        eng.wait_ge(sems[("d", g)], self.groups[g])

            block.tensor(lambda eng: run("pe", eng))
            block.scalar(lambda eng: run("act", eng))
            block.vector(lambda eng: run("dve", eng))
            block.gpsimd(lambda eng: run("pool", eng))
            block.sync(lambda eng: run("sp", eng))


class Arena:
    def __init__(self, nc, nbytes):
        self.t = nc.alloc_sbuf_tensor("arena", [128, nbytes // 2], BF16)
        self.top = 0
        self.cap = nbytes

    def alloc(self, nbytes):
        off = self.top
        self.top += (nbytes + BLK - 1) // BLK * BLK
        assert self.top <= self.cap, (self.top, self.cap)
        return off


class Buf:
    def __init__(self, arena, shape, dt, at=None):
        self.es = 4 if dt == F32 else 2
        n = int(np.prod(shape))
        self.nbytes = n * self.es
        self.off = arena.alloc(self.nbytes) if at is None else at
        v = arena.t[:, self.off // 2:(self.off + self.nbytes) // 2]
        if dt == F32:
            v = v.bitcast(F32)
        if len(shape) == 2:
            v = v.rearrange("p (a b) -> p a b", a=shape[0])
        elif len(shape) == 3:
            v = v.rearrange("p (a b c) -> p a b c", a=shape[0], b=shape[1])
        self.ap = v
        self.shape = tuple(shape)

    def k(self, lo=0, hi=None):
        if hi is None:
            hi = self.nbytes // self.es
        b0 = (self.off + lo * self.es) // BLK
        b1 = (self.off + hi * self.es - 1) // BLK
        return tuple(("sb", b) for b in range(b0, b1 + 1))

    def k2(self, i, lo=0, hi=None):
        n = int(np.prod(self.shape[1:]))
        if hi is None:
            hi = n
        return self.k(i * n + lo, i * n + hi)


def build(l0, l1, final):
    nc = bass.Bass("TRN2", target_bir_lowering=False)
    dr = lambda name, shape, dt, kind: nc.dram_tensor(name, shape, dt, kind=kind).ap()
    x_d = dr("x", [S, D], F32, "ExternalInput")
    out_d = dr("out", [S, D], F32, "ExternalOutput")
    w_in = dr("w_in", [NL, D, 9216], F32, "ExternalInput")
    w_pa = dr("w_pa", [NL, D, D], F32, "ExternalInput")
    w_pb = dr("w_pb", [NL, D, D], F32, "ExternalInput")
    w_o = dr("w_o", [NL, D, D], F32, "ExternalInput")
    w_s = dr("w_s", [NL, 8, 128, 128], F32, "ExternalInput")
    bs_d = dr("bs", [NL, 1, 1024], F32, "ExternalInput")
    gvb_d = dr("gvb", [NL, 128, 1024], F32, "ExternalInput")
    gn_d = dr("gn", [128, NL * 8], F32, "ExternalInput")
    gf_d = dr("gf", [128, 8], F32, "ExternalInput")
    id_d = dr("ident", [128, 128], F32, "ExternalInput")
    tri_d = dr("tri", [128, 128], F32, "ExternalInput")
    dist_d = dr("dist", [128, 256], F32, "ExternalInput")
    vd = dr("vd", [S, 8, 192], BF16, "Internal")

    P = Prog(nc)
    A = Arena(nc, 212480)
    XT = Buf(A, [8, S], F32)
    HT = Buf(A, [8, S], BF16)
    VY = Buf(A, [8, S], BF16)
    r_off = A.alloc(34 * 1024)
    QT = [Buf(A, [S], BF16, at=r_off + i * 4096) for i in range(2)]
    KT = [Buf(A, [S], BF16, at=r_off + 8192 + i * 4096) for i in range(2)]
    VL0 = [Buf(A, [16, 192], BF16, at=r_off + 16384)] * 2
    VL1 = Buf(A, [16, 192], BF16, at=r_off + 16384 + 6144)
    VL2 = Buf(A, [16, 192], BF16, at=r_off + 16384 + 12288)

    def v_stat(vbuf, tile, hh):
        return vbuf.ap[:, tile, 0:128] if hh == 0 else vbuf.ap[:, tile, 64:192]
    M = Buf(A, [8, S], BF16, at=r_off)
    QT4 = Buf(A, [S], BF16)
    WS_ = [Buf(A, [8, WCOLS], BF16) for _ in range(NSLOT)]
    GN = Buf(A, [NL * 8], F32)
    GF = Buf(A, [8], F32)
    GVB = Buf(A, [1024], F32)
    WSM = Buf(A, [8, 128], BF16)
    BSR = Buf(A, [1024], BF16)
    IDT = Buf(A, [128], F32)
    TRI = Buf(A, [128], F32)
    DIST = Buf(A, [256], F32)
    ONES = Buf(A, [128], BF16)
    t_off = A.alloc(12 * 1024)
    tmp = lambda shape, dt, o: Buf(A, shape, dt, at=t_off + o)
    XS = [Buf(A, [1024], F32, at=VY.off + i * 4096) for i in range(8)]
    SQ = [tmp([512], BF16, i * 1024) for i in range(2)]
    RSTD = [tmp([512], F32, 2048 + i * 2048) for i in range(2)]
    VST = [tmp([4, 192], BF16, i * 1536) for i in range(6)]
    PT = [tmp([512], BF16, i * 1024) for i in range(4)] + [tmp([512], BF16, 10752)]
    DD = [tmp([512], F32, 4096 + i * 2048) for i in range(2)]
    MK = tmp([2, 2, 256], BF16, 8192)
    MK3 = tmp([2, 128], BF16, 8192 + 2048)
    T1 = [tmp([512], BF16, i * 1024) for i in range(4)]
    GG = [tmp([1024], F32, i * 4096) for i in range(2)]
    SSQ = tmp([16], F32, 8192)
    UU2 = [tmp([S], BF16, i * 4096) for i in range(2)]
    SGT = [tmp([512], BF16, 8704 + i * 1024) for i in range(2)]
    WST = tmp([8, 128], F32, 8192)
    WSC = [tmp([128], BF16, 10752 + i * 512) for i in range(3)]
    wsc_i = [0]
    PS = [nc.alloc_psum_tensor("ps%d" % i, [128, 512], F32) for i in range(8)]
    psk = lambda b: (("ps", b),)

    def mm(out, lhsT, rhs, start, stop, reads, writes):
        P.op("pe", lambda e: e.matmul(out, lhsT=lhsT, rhs=rhs, start=start, stop=stop,
                                      skip_group_check=True), reads, writes)

    def act(out, in_, func, reads, writes, scale=1.0, accum_out=None):
        if accum_out is None:
            P.op("act", lambda e: e.activation(out=out, in_=in_, func=func, scale=scale), reads, writes)
        else:
            P.op("act", lambda e: e.activation(out=out, in_=in_, func=func, scale=scale,
                                               accum_out=accum_out), reads, writes)

    def tt(out, in0, in1, op, reads, writes, eng="dve"):
        P.op(eng, lambda e: e.tensor_tensor(out=out, in0=in0, in1=in1, op=op), reads, writes)

    def tcopy(out, in_, reads, writes, eng="dve"):
        P.op(eng, lambda e: e.tensor_copy(out=out, in_=in_), reads, writes)

    chunks = []
    for l in range(l0, l1):
        cw = lambda base, j, l=l: w_in[l, :, base + j * WCOLS: base + (j + 1) * WCOLS]
        for j in range(4):
            chunks.append(cw(5120, j))
        for j in range(4):
            chunks.append(cw(3072, j))
            chunks.append(cw(4096, j))
        for j in range(4):
            chunks.append(cw(6144, j))
        for j in range(4):
            chunks.append(w_pb[l, :, j * WCOLS:(j + 1) * WCOLS])
            chunks.append(cw(8192, j))
        for j in range(4):
            chunks.append(cw(1024, j))
        for j in range(4):
            chunks.append(cw(0, j))
            chunks.append(cw(2048, j))
        for j in range(4):
            chunks.append(w_pa[l, :, j * WCOLS:(j + 1) * WCOLS])
            chunks.append(cw(7168, j))
        for j in range(4):
            chunks.append(w_o[l, :, j * WCOLS:(j + 1) * WCOLS])
    wstate = dict(next_load=0, next_use=0)

    def w_load():
        i = wstate["next_load"]
        if i >= len(chunks):
            return
        wstate["next_load"] = i + 1
        slot = WS_[i % NSLOT]
        src = chunks[i].rearrange("(kt p) f -> p kt f", p=128)
        P.dma("pool", "w%d" % (i % NSLOT), lambda e: e.dma_start(out=slot.ap, in_=src),
              reads=(), writes=slot.k())

    def w_acquire():
        i = wstate["next_use"]
        wstate["next_use"] = i + 1
        return WS_[i % NSLOT]

    def w_release():
        w_load()

    P.dma("sp", "c0", lambda e: e.dma_start(out=GN.ap, in_=gn_d), writes=GN.k())
    P.dma("sp", "c1", lambda e: e.dma_start(out=GF.ap, in_=gf_d), writes=GF.k())
    P.dma("sp", "c2", lambda e: e.dma_start(out=IDT.ap, in_=id_d), writes=IDT.k())
    P.dma("sp", "c3", lambda e: e.dma_start(out=TRI.ap, in_=tri_d), writes=TRI.k())
    P.dma("sp", "c4", lambda e: e.dma_start(out=DIST.ap, in_=dist_d), writes=DIST.k())
    P.op("dve", lambda e: e.memset(ONES.ap, 1.0), writes=ONES.k())
    for _ in range(NSLOT):
        w_load()

    def rmsnorm_stats(s, rs, bank=None):
        if bank is None:
            bank = s % 4
        for ct in range(8):
            sq = SQ[ct % 2]
            act(sq.ap, XT.ap[:, ct, s * 512:(s + 1) * 512], AF.Square,
                XT.k2(ct, s * 512, (s + 1) * 512), sq.k())
            mm(PS[bank][:, :], ONES.ap, sq.ap, ct == 0, ct == 7, ONES.k() + sq.k(), psk(bank))
        P.op("dve", lambda e: e.tensor_scalar(out=rs.ap, in0=PS[bank][:, :], scalar1=1.0 / D, scalar2=EPS,
                                              op0=ALU.mult, op1=ALU.add), psk(bank), rs.k())
        act(rs.ap, rs.ap, AF.Ln, rs.k(), rs.k())
        act(rs.ap, rs.ap, AF.Exp, rs.k(), rs.k(), scale=-0.5)

    def proj_fm(wslot, c0, banks, evac):
        for s in range(4):
            for ct in range(8):
                mm(PS[banks[s]][:, :], wslot.ap[:, ct, c0:c0 + 128], HT.ap[:, ct, s * 512:(s + 1) * 512],
                   ct == 0, ct == 7, wslot.k2(ct) + HT.k2(ct, s * 512, (s + 1) * 512), psk(banks[s]))
            evac(s, banks[s])

    def p0_span(l_next, s, ssbank):
        rs = RSTD[s % 2]
        rmsnorm_stats(s, rs, ssbank)
        for ct in range(8):
            xk = XT.k2(ct, s * 512, (s + 1) * 512)
            if l_next is not None:
                P.op("dve", lambda e, ct=ct, rs=rs: e.scalar_tensor_tensor(
                    out=HT.ap[:, ct, s * 512:(s + 1) * 512], in0=XT.ap[:, ct, s * 512:(s + 1) * 512],
                    scalar=GN.ap[:, (l_next * 8 + ct):(l_next * 8 + ct) + 1], in1=rs.ap, op0=ALU.mult, op1=ALU.mult),
                    xk + GN.k() + rs.k(), HT.k2(ct, s * 512, (s + 1) * 512))
            else:
                P.op("dve", lambda e, ct=ct, rs=rs: e.scalar_tensor_tensor(
                    out=XT.ap[:, ct, s * 512:(s + 1) * 512], in0=XT.ap[:, ct, s * 512:(s + 1) * 512],
                    scalar=GF.ap[:, ct:ct + 1], in1=rs.ap, op0=ALU.mult, op1=ALU.mult), xk + GF.k() + rs.k(), xk)

    for i in range(16):
        xs = XS[i % 8]
        P.dma("sp", "xs%d" % (i % 8), lambda e, xs=xs, i=i: e.dma_start(out=xs.ap, in_=x_d[i * 128:(i + 1) * 128, :]),
              writes=xs.k())
        for half in range(2):
            bank = (i * 2 + half) % 4
            for c4 in range(4):
                ct = half * 4 + c4
                P.op("pe", lambda e, bank=bank, c4=c4, xs=xs, ct=ct: e.transpose(
                    PS[bank][:, c4 * 128:(c4 + 1) * 128], xs.ap[:, ct * 128:(ct + 1) * 128], IDT.ap),
                    reads=xs.k(ct * 128, (ct + 1) * 128) + IDT.k(), writes=psk(bank))
            dst = XT.ap[:, half * 4:half * 4 + 4, i * 128:(i + 1) * 128]
            src = PS[bank][:, :].rearrange("p (a b) -> p a b", a=4)
            wk = ()
            for c4 in range(4):
                wk += XT.k2(half * 4 + c4, i * 128, (i + 1) * 128)
            if half == 0:
                tcopy(dst, src, psk(bank), wk, eng="dve")
            else:
                act(dst, src, AF.Copy, psk(bank), wk)
        if i % 4 == 3:
            p0_span(l0, i // 4, 4 + (i // 4) % 4)

    slopes = [2.0 ** (-8.0 * (h + 1) / NH) for h in range(NH)]
    bankset = [0]

    def next_banks():
        b = bankset[0]
        bankset[0] = 4 - b
        return [b, b + 1, b + 2, b + 3]

    for l in range(l0, l1):
        P.dma("sp", "lp0", lambda e, l=l: e.dma_start(out=GVB.ap, in_=gvb_d[l]), writes=GVB.k())
        P.dma("pool", "lpb", lambda e, l=l: e.dma_start(out=BSR.ap[0:1, :], in_=bs_d[l]), writes=BSR.k())
        for half in range(2):
            wv2 = [w_acquire(), w_acquire()]
            for i in range(16):
                bank = 4 + i % 4
                vst = VST[i % 6]
                if half == 0 and i < 6:
                    P.op("dve", lambda e, vst=vst: e.memset(vst.ap[:, :, 64:128], 1.0), writes=vst.k())
                for j in range(2):
                    for ct in range(8):
                        mm(PS[bank][:, j * 256:(j + 1) * 256], HT.ap[:, ct, i * 128:(i + 1) * 128],
                           wv2[j].ap[:, ct, :], ct == 0, ct == 7,
                           HT.k2(ct, i * 128, (i + 1) * 128) + wv2[j].k2(ct), psk(bank))
                src = PS[bank][:, :].rearrange("p (a h c) -> p a h c", a=4, h=2)
                dst = vst.ap.rearrange("p a (h c) -> p a h c", h=3)[:, :, 0:3:2, :]
                if i % 2 == 0:
                    tcopy(dst, src, psk(bank), vst.k())
                else:
                    act(dst, src, AF.Copy, psk(bank), vst.k())
                P.dma("sp", "vst%d" % (i % 6), lambda e, vst=vst, i=i, half=half: e.dma_start(
                    out=vd[i * 128:(i + 1) * 128, half * 4:(half + 1) * 4, :], in_=vst.ap),
                    reads=vst.k(), writes=tuple(("vd", hp_, i) for hp_ in range(half * 4, half * 4 + 4)))
            w_release()
            w_release()

        def load_v0(hp):
            rk = tuple(("vd", hp, i_) for i_ in range(16))
            vb = VL0[hp % 2]
            P.dma("sp", "vl0", lambda e: e.dma_start(
                out=vb.ap, in_=vd[:, hp, :].rearrange("(i p) c -> p i c", p=128)), reads=rk, writes=vb.k())

        def load_v1(hp):
            rk = tuple(("vd", hp, i_) for i_ in range(16))
            P.dma("sp", "vl1", lambda e: e.dma_start(
                out=VL1.ap.rearrange("p (r b) c -> p r b c", r=4),
                in_=vd[:, hp, :].rearrange("(b p r) c -> p r b c", b=4, p=128, r=4)), reads=rk, writes=VL1.k())

        def load_v2(hp):
            rk = tuple(("vd", hp, i_) for i_ in range(16))
            P.dma("sp", "vl2", lambda e: e.dma_start(
                out=VL2.ap, in_=vd[:, hp, :].rearrange("(p r) c -> p r c", r=16)), reads=rk, writes=VL2.k())

        qk_slots = {}

        def qk_item(hp, which, s, qb=7):
            j = hp // 2
            if (hp % 2 == 0) and which == 0 and s == 0:
                qk_slots[j] = (w_acquire(), w_acquire())
            wsl = qk_slots[j][which]
            c0 = (hp % 2) * 128
            dst = (QT if which == 0 else KT)[hp % 2]
            for ct in range(8):
                mm(PS[qb][:, :], wsl.ap[:, ct, c0:c0 + 128], HT.ap[:, ct, s * 512:(s + 1) * 512],
                   ct == 0, ct == 7, wsl.k2(ct) + HT.k2(ct, s * 512, (s + 1) * 512), psk(qb))
            tcopy(dst.ap[:, s * 512:(s + 1) * 512], PS[qb][:, :], psk(qb), dst.k(s * 512, (s + 1) * 512))
            if (hp % 2 == 1) and which == 1 and s == 3:
                w_release()
                w_release()

        items = [(w, s) for s in range(4) for w in range(2)]
        for ii, (w, s) in enumerate(items):
            qk_item(0, w, s, ii)

        load_v0(0)
        load_v2(0)
        load_v1(0)
        tasks = []
        first_flag = {}
        for hp in range(8):
            qt, kt = QT[hp % 2], KT[hp % 2]
            vl0 = VL0[hp % 2]

            def gen_masks(hp=hp, qt=qt):
                P.op("dve", lambda e, qt=qt: e.tensor_copy(
                    out=QT4.ap.rearrange("p (b r j) -> p b r j", b=4, r=4),
                    in_=qt.ap.rearrange("p (b j r) -> p b r j", b=4, r=4)), qt.k(), QT4.k())
                for hh in range(2):
                    sl = slopes[2 * hp + hh]
                    for pi, dil in enumerate((1, 4)):
                        act(MK.ap[:, hh, pi, :], DIST.ap, AF.Exp, DIST.k(), MK.k(), scale=-sl * dil)
                    act(MK3.ap[:, hh, :], DIST.ap[:, 128:256], AF.Exp, DIST.k(), MK3.k(), scale=-sl * 16)
            ptasks = []
            for b in range(4):
                obank = {0: (b % 2) * 2, 1: (b % 2) * 2 + 1}
                for hh in range(2):
                    first_flag[(hp, b, hh)] = True

                def pv(hh, vbuf, tile, ptile, pcols, ocols, obank=obank, hp=hp, b=b):
                    ob = obank[hh]
                    fkey = (hp, b, hh)
                    mm(PS[ob][:, ocols], v_stat(vbuf, tile, hh), ptile.ap[:, pcols],
                       first_flag[fkey], False, vbuf.k() + ptile.k(), psk(ob))
                    first_flag[fkey] = False

                work = []
                for jq in range(4):
                    j = 4 * b + jq
                    qsl = slice(j * 128, (j + 1) * 128)
                    ksl_prev = slice((j - 1) * 128, j * 128) if j > 0 else None
                    work.append((0, qsl, ksl_prev, qsl, (vl0, j - 1), (vl0, j), slice(jq * 128, (jq + 1) * 128)))
                for r in range(4):
                    qsl = slice(512 * b + r, 512 * (b + 1), 4)
                    ksl_prev = slice(512 * (b - 1) + r, 512 * b, 4) if b > 0 else None
                    work.append((1, slice(512 * b + 128 * r, 512 * b + 128 * (r + 1)), ksl_prev, qsl,
                                 (VL1, r * 4 + b - 1), (VL1, r * 4 + b), slice(r, 512, 4)))
                for wi in range(0, 8, 2):
                    for hh in range(2):
                        def s_fn(tk, wi=wi, work=work, b=b, qt=qt, kt=kt, hh=hh):
                            sb, ptile = tk["sb"], tk["pt"]
                            pb = hh * 64
                            pat = work[wi][0]
                            for u in range(2):
                                _, qsl, kprev, kcur, vprev, vcur, ocols = work[wi + u]
                                qsrc = QT4 if pat == 1 else qt
                                qrd = qsrc.k(qsl.start, qsl.stop)
                                if kprev is not None:
                                    mm(PS[sb][:, u * 256:u * 256 + 128], kt.ap[pb:pb + 64, kprev], qsrc.ap[pb:pb + 64, qsl],
                                       True, True, kt.k(max(0, 512 * (b - 1)), 512 * (b + 1)) + qrd, psk(sb))
                                mm(PS[sb][:, u * 256 + 128:u * 256 + 256], kt.ap[pb:pb + 64, kcur], qsrc.ap[pb:pb + 64, qsl],
                                   True, True, kt.k(512 * b, 512 * (b + 1)) + qrd, psk(sb))
                            act(ptile.ap, PS[sb][:, :], AF.Exp, psk(sb), ptile.k(), scale=0.125)
                            mk = MK.ap[:, hh, pat, :].rearrange("p (o c) -> p o c", o=1).broadcast_to([128, 2, 256])
                            p3 = ptile.ap.rearrange("p (a c) -> p a c", a=2)
                            tt(p3, p3, mk, ALU.mult, ptile.k() + MK.k(), ptile.k())

                        def pv_fn(tk, wi=wi, work=work, pv=pv, hh=hh):
                            ptile = tk["pt"]
                            for u in range(2):
                                _, qsl, kprev, kcur, vprev, vcur, ocols = work[wi + u]
                                if kprev is not None:
                                    pv(hh, vprev[0], vprev[1], ptile, slice(u * 256, u * 256 + 128), ocols)
                                pv(hh, vcur[0], vcur[1], ptile, slice(u * 256 + 128, u * 256 + 256), ocols)
                        tkw = dict(s=s_fn, pv=pv_fn, before=[], after=[])
                        ptasks.append(tkw)
                        if b == 3 and wi == 2 and hh == 1 and hp + 1 < 8:
                            tkw["after"].append(lambda hp=hp: load_v0(hp + 1))

                for hh in range(2):
                    def s3_fn(tk, hh=hh, b=b, qt=qt, kt=kt):
                        sb, ptile = tk["sb"], tk["pt"]
                        pb = hh * 64
                        for r in range(16):
                            mm(PS[sb][:, r * 32:(r + 1) * 32], kt.ap[pb:pb + 64, r:S:16],
                               qt.ap[pb:pb + 64, 512 * b + r:512 * (b + 1):16], True, True,
                               kt.k() + qt.k(512 * b, 512 * (b + 1)), psk(sb))
                        act(ptile.ap, PS[sb][:, :], AF.Exp, psk(sb), ptile.k(), scale=0.125)
                        mk = MK3.ap[:, hh, 32 * b:32 * (b + 1)].rearrange("p (o c) -> p o c", o=1).broadcast_to([128, 16, 32])
                        p3 = ptile.ap.rearrange("p (a c) -> p a c", a=16)
                        tt(p3, p3, mk, ALU.mult, ptile.k() + MK3.k(), ptile.k())

                    def pv3_fn(tk, hh=hh, pv=pv):
                        ptile = tk["pt"]
                        for r in range(16):
                            pv(hh, VL2, r, ptile, slice(r * 32, (r + 1) * 32), slice(r, 512, 16))
                    tk3 = dict(s=s3_fn, pv=pv3_fn, before=[], after=[])
                    ptasks.insert(len(ptasks) - (4 if (hh == 0 or b == 3) else 0), tk3)
                    if b == 3 and hh == 1 and hp + 1 < 8:
                        tk3["after"].append(lambda hp=hp: load_v2(hp + 1))

                def norm_fn(b=b, obank=obank, hp=hp):
                    dd = DD[b % 2]
                    oa, obb = obank[0], obank[1]
                    act(dd.ap[0:64, :], PS[oa][64:128, :], AF.Ln, psk(oa), dd.k())
                    act(dd.ap[64:128, :], PS[obb][0:64, :], AF.Ln, psk(obb), dd.k())
                    act(dd.ap, dd.ap, AF.Exp, dd.k(), dd.k(), scale=-1.0)
                    tt(VY.ap[0:64, hp, b * 512:(b + 1) * 512], PS[oa][0:64, :], dd.ap[0:64, :], ALU.mult,
                       psk(oa) + dd.k(), VY.k2(hp, b * 512, (b + 1) * 512))
                    tt(VY.ap[64:128, hp, b * 512:(b + 1) * 512], PS[obb][64:128, :], dd.ap[64:128, :], ALU.mult,
                       psk(obb) + dd.k(), VY.k2(hp, b * 512, (b + 1) * 512))
                ptasks[-1]["after"].append(norm_fn)
            ptasks[0]["before"].append(gen_masks)
            if hp + 1 < 8:
                for ii, (w_, s_) in reversed(list(enumerate(items))):
                    ptasks.insert(5 * ii + 5, dict(
                        s=lambda tk, hp=hp, w_=w_, s_=s_: qk_item(hp + 1, w_, s_, tk["sb"]),
                        pv=lambda tk: None, before=[], after=[]))
                ptasks[-1]["after"].append(lambda hp=hp: load_v1(hp + 1))
            tasks += ptasks

        SKEW = 3
        for t in range(len(tasks) + SKEW):
            if t < len(tasks):
                tk = tasks[t]
                tk["sb"] = 4 + t % 4
                tk["pt"] = PT[t % 5]
                for f in tk["before"]:
                    f()
                tk["s"](tk)
            if t >= SKEW:
                tk = tasks[t - SKEW]
                tk["pv"](tk)
                for f in tk["after"]:
                    f()

        ti = [0]
        for j in range(4):
            wsl = w_acquire()
            for f2 in range(2):
                hp = 2 * j + f2

                def ev(s, bank, hp=hp):
                    t1 = T1[ti[0] % 4]
                    ti[0] += 1
                    act(t1.ap, PS[bank][:, :], AF.Silu, psk(bank), t1.k())
                    tt(VY.ap[:, hp, s * 512:(s + 1) * 512], VY.ap[:, hp, s * 512:(s + 1) * 512], t1.ap, ALU.mult,
                       VY.k2(hp, s * 512, (s + 1) * 512) + t1.k(), VY.k2(hp, s * 512, (s + 1) * 512))
                proj_fm(wsl, f2 * 128, next_banks(), ev)
            w_release()

        def proj_merge(accumulate):
            for j in range(4):
                wp = w_acquire()
                wg = w_acquire()
                for f2 in range(2):
                    n = 2 * j + f2
                    gb = next_banks()
                    sig = []

                    def ev_g(s, bank):
                        t1 = T1[ti[0] % 4]
                        ti[0] += 1
                        act(t1.ap, PS[bank][:, :], AF.Sigmoid, psk(bank), t1.k())
                        sig.append(t1)
                    pbanks = next_banks()
                    for s in range(4):
                        for ct in range(8):
                            mm(PS[gb[s]][:, :], wg.ap[:, ct, f2 * 128:(f2 + 1) * 128], HT.ap[:, ct, s * 512:(s + 1) * 512],
                               ct == 0, ct == 7, wg.k2(ct) + HT.k2(ct, s * 512, (s + 1) * 512), psk(gb[s]))
                        ev_g(s, gb[s])
                        for ct in range(8):
                            mm(PS[pbanks[s]][:, :], wp.ap[:, ct, f2 * 128:(f2 + 1) * 128], VY.ap[:, ct, s * 512:(s + 1) * 512],
                               ct == 0, ct == 7, wp.k2(ct) + VY.k2(ct, s * 512, (s + 1) * 512), psk(pbanks[s]))
                        t1 = sig[s]
                        mk_ = M.k2(n, s * 512, (s + 1) * 512)
                        if not accumulate:
                            tt(M.ap[:, n, s * 512:(s + 1) * 512], PS[pbanks[s]][:, :], t1.ap, ALU.mult,
                               psk(pbanks[s]) + t1.k(), mk_)
                        else:
                            tt(t1.ap, PS[pbanks[s]][:, :], t1.ap, ALU.mult, psk(pbanks[s]) + t1.k(), t1.k())
                            tt(M.ap[:, n, s * 512:(s + 1) * 512], M.ap[:, n, s * 512:(s + 1) * 512], t1.ap, ALU.add,
                               mk_ + t1.k(), mk_)
                w_release()
                w_release()

        P.dma("sp", "lp1", lambda e, l=l: e.dma_start(out=WST.ap, in_=w_s[l].rearrange("g t s -> t g s")),
              writes=WST.k())
        proj_merge(False)

        for g in range(8):
            bank = 4 + (g % 4)
            P.op("pe", lambda e, g=g, bank=bank: e.transpose(PS[bank][:, 0:128], WST.ap[:, g, :], IDT.ap),
                 WST.k() + IDT.k(), psk(bank))
            tt(WSM.ap[:, g, :], PS[bank][:, 0:128], TRI.ap, ALU.mult, psk(bank) + TRI.k(), WSM.k2(g))

        wsl4 = [w_acquire() for _ in range(4)]
        P.op("dve", lambda e: e.memset(SSQ.ap, 0.0), writes=SSQ.k())
        VN = VY.ap.rearrange("p g (c f) -> p g c f", c=16)
        for i in range(16):
            gg = GG[i % 2]
            banks = (4, 5) if i % 2 == 0 else (6, 7)
            for j in range(4):
                bank = banks[j // 2]
                for ct in range(8):
                    mm(PS[bank][:, (j % 2) * 256:(j % 2 + 1) * 256], HT.ap[:, ct, i * 128:(i + 1) * 128],
                       wsl4[j].ap[:, ct, :], ct == 0, ct == 7,
                       HT.k2(ct, i * 128, (i + 1) * 128) + wsl4[j].k2(ct), psk(bank))
            for hb in range(2):
                act(gg.ap[:, hb * 512:(hb + 1) * 512], PS[banks[hb]][:, :], AF.Gelu_apprx_tanh, psk(banks[hb]),
                    gg.k(hb * 512, (hb + 1) * 512))
            junk = tmp([1024], BF16, 8704)
            act(junk.ap, gg.ap, AF.Square, gg.k(), junk.k() + SSQ.k(), accum_out=SSQ.ap[:, i:i + 1])
            wk = ()
            for g in range(8):
                wk += VY.k2(g, i * 128, (i + 1) * 128)
            tt(VN[:, :, i, :], gg.ap.rearrange("p (g f) -> p g f", g=8), GVB.ap.rearrange("p (g f) -> p g f", g=8),
               ALU.mult, gg.k() + GVB.k(), wk)
        P.op("dve", lambda e: e.tensor_scalar(out=SSQ.ap, in0=SSQ.ap, scalar1=1.0 / D, scalar2=EPS,
                                              op0=ALU.mult, op1=ALU.add), SSQ.k(), SSQ.k())
        act(SSQ.ap, SSQ.ap, AF.Ln, SSQ.k(), SSQ.k())
        act(SSQ.ap, SSQ.ap, AF.Exp, SSQ.k(), SSQ.k(), scale=-0.5)
        for c in range(16):
            wk = ()
            for g in range(8):
                wk += VY.k2(g, c * 128, (c + 1) * 128)
            P.op("dve", lambda e, c=c: e.tensor_scalar(out=VN[:, :, c, :], in0=VN[:, :, c, :], scalar1=SSQ.ap[:, c:c + 1],
                                                       scalar2=None, op0=ALU.mult), wk + SSQ.k(), wk)
        for _ in range(4):
            w_release()

        ub = [0, 1, 2, 3]
        gbk = [4, 5, 6, 7]
        a2w = {}
        sgt_i = [0]

        def a2_w(kind, g):
            key = (kind, g // 2)
            if key not in a2w:
                a2w[key] = w_acquire()
            return a2w[key]

        def a2_U(g):
            wu = a2_w("u", g)
            f2 = g % 2
            uu = UU2[g % 2]
            for s in range(4):
                for ct in range(8):
                    mm(PS[ub[s]][:, :], wu.ap[:, ct, f2 * 128:(f2 + 1) * 128], HT.ap[:, ct, s * 512:(s + 1) * 512],
                       ct == 0, ct == 7, wu.k2(ct) + HT.k2(ct, s * 512, (s + 1) * 512), psk(ub[s]))
                act(uu.ap[:, s * 512:(s + 1) * 512], PS[ub[s]][:, :], AF.Gelu_apprx_tanh, psk(ub[s]),
                    uu.k(s * 512, (s + 1) * 512))
            if f2 == 1:
                w_release()

        def a2_G(g):
            wgt = a2_w("g", g)
            f2 = g % 2
            uu = UU2[g % 2]
            for s in range(4):
                for ct in range(8):
                    mm(PS[gbk[s]][:, :], wgt.ap[:, ct, f2 * 128:(f2 + 1) * 128], HT.ap[:, ct, s * 512:(s + 1) * 512],
                       ct == 0, ct == 7, wgt.k2(ct) + HT.k2(ct, s * 512, (s + 1) * 512), psk(gbk[s]))
                sgt = SGT[sgt_i[0] % 2]
                sgt_i[0] += 1
                uus = uu.ap[:, s * 512:(s + 1) * 512]
                uuk = uu.k(s * 512, (s + 1) * 512)
                act(sgt.ap, PS[gbk[s]][:, :], AF.Tanh, psk(gbk[s]), sgt.k(), scale=0.5)
                P.op("dve", lambda e, sgt=sgt, s=s: e.scalar_tensor_tensor(
                    out=sgt.ap, in0=sgt.ap, scalar=1.0, in1=PS[gbk[s]][:, :], op0=ALU.add, op1=ALU.mult),
                    sgt.k() + psk(gbk[s]), sgt.k())
                P.op("dve", lambda e, sgt=sgt, uus=uus: e.scalar_tensor_tensor(
                    out=uus, in0=uus, scalar=0.5, in1=sgt.ap, op0=ALU.mult, op1=ALU.mult), uuk + sgt.k(), uuk)
            if f2 == 1:
                w_release()

        def a2_M(g):
            uu = UU2[g % 2]
            for s in range(4):
                brow = BSR.ap[0:1, g * 128:(g + 1) * 128].rearrange("p (o c) -> p o c", o=1).broadcast_to([1, 4, 128])
                mm(PS[gbk[s]][:, :].rearrange("p (a c) -> p a c", a=4), ONES.ap[0:1, :], brow, True, False,
                   ONES.k() + BSR.k(), psk(gbk[s]))
                for c4 in range(4):
                    c = s * 4 + c4
                    cols = slice(c4 * 128, (c4 + 1) * 128)
                    mm(PS[gbk[s]][:, cols], VN[:, g, c, :], WSM.ap[:, g, :], False, c4 == 3,
                       VY.k2(g, c * 128, (c + 1) * 128) + WSM.k2(g), psk(gbk[s]))
                tt(VY.ap[:, g, s * 512:(s + 1) * 512], PS[gbk[s]][:, :], uu.ap[:, s * 512:(s + 1) * 512], ALU.mult,
                   psk(gbk[s]) + uu.k(s * 512, (s + 1) * 512), VY.k2(g, s * 512, (s + 1) * 512))

        a2_U(0)
        a2_G(0)
        for g in range(8):
            if g + 1 < 8:
                a2_U(g + 1)
            a2_M(g)
            if g + 1 < 8:
                a2_G(g + 1)
        bankset[0] = 0

        proj_merge(True)

        wo4 = [w_acquire() for _ in range(4)]
        wb = 0
        for s in range(4):
            for m in range(8):
                j, f2 = m // 2, m % 2
                wo = wo4[j]
                bank = wb % 6
                wb += 1
                for n in range(8):
                    mm(PS[bank][:, :], wo.ap[:, n, f2 * 128:(f2 + 1) * 128], M.ap[:, n, s * 512:(s + 1) * 512],
                       n == 0, n == 7, wo.k2(n) + M.k2(n, s * 512, (s + 1) * 512), psk(bank))
                xk = XT.k2(m, s * 512, (s + 1) * 512)
                tt(XT.ap[:, m, s * 512:(s + 1) * 512], PS[bank][:, :], XT.ap[:, m, s * 512:(s + 1) * 512], ALU.add,
                   psk(bank) + xk, xk)
                if s == 3 and f2 == 1:
                    w_release()
            if l + 1 < l1:
                p0_span(l + 1, s, 6 + s % 2)
            elif final:
                p0_span(None, s, 6 + s % 2)

    OS = [Buf(A, [1024], F32, at=VY.off + i * 4096) for i in range(8)]
    for i in range(16):
        os_ = OS[i % 8]
        for half in range(2):
            bank = 4 + (i * 2 + half) % 4
            for c4 in range(4):
                ct = half * 4 + c4
                P.op("pe", lambda e, bank=bank, c4=c4, ct=ct, i=i: e.transpose(
                    PS[bank][:, c4 * 128:(c4 + 1) * 128], XT.ap[:, ct, i * 128:(i + 1) * 128], IDT.ap),
                    XT.k2(ct, i * 128, (i + 1) * 128) + IDT.k(), psk(bank))
            if half == 0:
                tcopy(os_.ap[:, 0:512], PS[bank][:, :], psk(bank), os_.k(0, 512))
            else:
                act(os_.ap[:, 512:1024], PS[bank][:, :], AF.Copy, psk(bank), os_.k(512, 1024))
        P.dma("sp", "os%d" % (i % 8), lambda e, os_=os_, i=i: e.dma_start(out=out_d[i * 128:(i + 1) * 128, :], in_=os_.ap),
              reads=os_.k(), writes=(("out", i),))
    P.emit(final_waits=tuple("os%d" % i for i in range(8)))
    return nc


_CONST = {}


def _consts():
    if _CONST:
        return _CONST
    k = np.arange(128)[:, None]
    q = np.arange(128)[None, :]
    prev = np.where(q <= k, 128 + q - k, BIG)
    cur = np.where(q >= k, q - k, BIG)
    _CONST["dist"] = np.concatenate([prev, cur], axis=1).astype(np.float32)
    _CONST["tri"] = (q >= k).astype(np.float32)
    _CONST["ident"] = np.eye(128, dtype=np.float32)
    return _CONST


_PROGS = {}


def _prog(l0, l1, final):
    key = (l0, l1, final)
    if key not in _PROGS:
        _PROGS[key] = build(l0, l1, final)
    return _PROGS[key]


FUSED = True


def kernel(x, g_norm, w_in, w_s, b_s, g_v, w_proj_a, w_proj_b, w_out, g_final):
    c = _consts()
    f32 = lambda a: np.ascontiguousarray(np.asarray(a, dtype=np.float32))
    x = f32(x)
    shared = {
        "w_in": f32(w_in), "w_pa": f32(w_proj_a), "w_pb": f32(w_proj_b), "w_o": f32(w_out),
        "w_s": f32(w_s), "bs": f32(b_s).reshape(NL, 1, 1024),
        "gvb": np.ascontiguousarray(np.broadcast_to(f32(g_v)[:, None, :], (NL, 128, 1024))),
        "gn": np.ascontiguousarray(f32(g_norm).reshape(NL, 8, 128).transpose(2, 0, 1).reshape(128, NL * 8)),
        "gf": np.ascontiguousarray(f32(g_final).reshape(8, 128).T),
        "ident": c["ident"], "tri": c["tri"], "dist": c["dist"],
    }
    n = x.shape[0]
    if FUSED:
        stages = [(0, NL, True)]
    else:
        stages = [(l, l + 1, l == NL - 1) for l in range(NL)]
    cur = [x[i] for i in range(n)]
    for (l0, l1, fin) in stages:
        nc = _prog(l0, l1, fin)
        in_maps = [dict(shared, x=np.ascontiguousarray(cur[i])) for i in range(n)]
        res = run_bass_kernel_spmd(nc, in_maps, core_ids=list(range(n)))
        cur = [np.asarray(res.results[i]["out"], dtype=np.float32) for i in range(n)]
    return np.stack(cur, axis=0)
```
